# Optimizing a Trainium2 kernel written in Bass

```python
import math
import jax
import jax.numpy as jnp
from jax import lax
import numpy as np

D_MODEL = 1024
BATCH = 8
SEQ = 8192
DEPTH = 2

GRID_W = 64
CTX_LEN = 256
EPS = 1e-6
MLA_HEADS = 8
MLA_Q_RANK = 256
MLA_KV_RANK = 256
MLA_NOPE = 64
MLA_ROPE = 32
MLA_V = 64
MLA_WIDTH = MLA_HEADS * MLA_V
MLA_SCALE = (MLA_NOPE + MLA_ROPE) ** -0.5
ROPE_FREQS = MLA_ROPE // 4
ROPE_BASE = 10000.0
Q_BLOCK = 128
SSM_WIDTH = 512
SSM_GROUP = 16
SSM_GROUPS = SSM_WIDTH // SSM_GROUP
SSM_STATE = 64
DT_MIN = 0.001
DT_MAX = 0.1
GLA_HEADS = 4
GLA_DK = 64
GLA_DV = 128
GLA_WIDTH = GLA_HEADS * GLA_DV
GLA_GATE_RANK = 16
GLA_GATE_TAU = 16.0
GLA_CHUNK = 64
N_BRANCH = 3
FFN_HIDDEN = -(-(8 * D_MODEL) // (3 * 256)) * 256
IN_SPLITS = (MLA_Q_RANK, MLA_KV_RANK, MLA_ROPE, SSM_WIDTH, GLA_HEADS * GLA_DK, GLA_HEADS * GLA_DK, GLA_WIDTH, GLA_WIDTH, GLA_GATE_RANK, GLA_GATE_RANK, N_BRANCH * D_MODEL)
IN_WIDTH = sum(IN_SPLITS)

kernel_name = 'hybrid_mla_s5_gla_prefix_dit'


def rms_norm(x, g):
    x32 = x.astype(jnp.float32)
    y = x32 * lax.rsqrt(jnp.mean(x32 * x32, axis=-1, keepdims=True) + EPS)
    return (y * g.astype(jnp.float32)).astype(x.dtype)


def modulate(x, g, shift, scale):
    return rms_norm(x, g) * (1 + scale) + shift


def split_cols(z, sizes):
    cuts, acc = [], 0
    for s in sizes[:-1]:
        acc += s
        cuts.append(acc)
    return jnp.split(z, cuts, axis=-1)


def axial_angles(rows_n):
    rows = jnp.repeat(jnp.arange(rows_n, dtype=jnp.float32), GRID_W)
    cols = jnp.tile(jnp.arange(GRID_W, dtype=jnp.float32), rows_n)
    inv = ROPE_BASE ** (-jnp.arange(ROPE_FREQS, dtype=jnp.float32) / ROPE_FREQS)
    ang = jnp.concatenate([rows[:, None] * inv, cols[:, None] * inv], axis=-1)
    return jnp.cos(ang), jnp.sin(ang)


def apply_rope(x, cos, sin):
    xp = x.reshape(x.shape[:-1] + (x.shape[-1] // 2, 2))
    x0, x1 = xp[..., 0], xp[..., 1]
    cos = cos.astype(x.dtype)
    sin = sin.astype(x.dtype)
    return jnp.stack([x0 * cos - x1 * sin, x0 * sin + x1 * cos], axis=-1).reshape(x.shape)


def mla_queries(cq, p):
    B, T, _ = cq.shape
    q = (rms_norm(cq, p['mla_q_norm']) @ p['mla_w_uq']).reshape(B, T, MLA_HEADS, MLA_NOPE + MLA_ROPE)
    return q[..., :MLA_NOPE], q[..., MLA_NOPE:]


def mla_keys(ckv, p):
    B, T, _ = ckv.shape
    kv = (rms_norm(ckv, p['mla_kv_norm']) @ p['mla_w_ukv']).reshape(B, T, MLA_HEADS, MLA_NOPE + MLA_V)
    return kv[..., :MLA_NOPE], kv[..., MLA_NOPE:]


def mla_attend(qn, qr, kn, kr, v):
    s = jnp.einsum('bqhd,bkhd->bhqk', qn, kn) + jnp.einsum('bqhr,bkr->bhqk', qr, kr)
    prob = jax.nn.softmax(s.astype(jnp.float32) * MLA_SCALE, axis=-1).astype(v.dtype)
    return jnp.einsum('bhqk,bkhv->bqhv', prob, v)


def mla_branch(zl, zc, p, cos, sin, with_ctx):
    cq, ckv, kr = zl
    cqc, ckvc, krc = zc
    B, T, _ = cq.shape
    qn, qr = mla_queries(cq, p)
    qr = apply_rope(qr, cos[:, None, :], sin[:, None, :])
    kn, v = mla_keys(ckv, p)
    kr = apply_rope(kr, cos, sin)
    knc, vc = mla_keys(ckvc, p)
    kn_all = jnp.concatenate([knc, kn], axis=1)
    kr_all = jnp.concatenate([krc, kr], axis=1)
    v_all = jnp.concatenate([vc, v], axis=1)
    nb = T // Q_BLOCK

    def blocks(t):
        return jnp.moveaxis(t.reshape((B, nb, Q_BLOCK) + t.shape[2:]), 1, 0)

    o = lax.map(lambda qb: mla_attend(qb[0], qb[1], kn_all, kr_all, v_all), (blocks(qn), blocks(qr)))
    y = jnp.moveaxis(o, 0, 1).reshape(B, T, MLA_WIDTH) @ p['mla_w_o']
    if not with_ctx:
        return y, None
    qnc, qrc = mla_queries(cqc, p)
    oc = mla_attend(qnc, qrc, knc, krc, vc).reshape(B, cqc.shape[1], MLA_WIDTH)
    return y, oc @ p['mla_w_o']


def s5_discretise(lam_re, lam_im, log_dt, b_re, b_im):
    lr = lam_re.astype(jnp.float32)
    li = lam_im.astype(jnp.float32)
    dt = jnp.exp(log_dt.astype(jnp.float32))[:, None]
    mag = jnp.exp(dt * lr)
    ar = mag * jnp.cos(dt * li)
    ai = mag * jnp.sin(dt * li)
    den = lr * lr + li * li
    fr = ((ar - 1) * lr + ai * li) / den
    fi = (ai * lr - (ar - 1) * li) / den
    br = b_re.astype(jnp.float32)
    bi = b_im.astype(jnp.float32)
    bbr = fr[..., None] * br - fi[..., None] * bi
    bbi = fr[..., None] * bi + fi[..., None] * br
    return ar, ai, bbr, bbi


def complex_affine_combine(e1, e2):
    a1r, a1i, b1r, b1i = e1
    a2r, a2i, b2r, b2i = e2
    return (a2r * a1r - a2i * a1i, a2r * a1i + a2i * a1r,
            a2r * b1r - a2i * b1i + b2r, a2r * b1i + a2i * b1r + b2i)


def s5_scan(u, ar, ai, bbr, bbi, h0r, h0i, reverse):
    T = u.shape[1]
    xr = jnp.einsum('btgh,gph->btgp', u, bbr)
    xi = jnp.einsum('btgh,gph->btgp', u, bbi)
    if h0r is not None:
        edge = T - 1 if reverse else 0
        xr = xr.at[:, edge].add(ar * h0r - ai * h0i)
        xi = xi.at[:, edge].add(ar * h0i + ai * h0r)
    a_r = jnp.broadcast_to(ar, (1, T) + ar.shape)
    a_i = jnp.broadcast_to(ai, (1, T) + ai.shape)
    _, _, sr, si = lax.associative_scan(complex_affine_combine, (a_r, a_i, xr, xi), reverse=reverse, axis=1)
    return sr, si


def s5_readout(sr, si, c_re, c_im):
    return jnp.einsum('btgp,ghp->btgh', sr, c_re) - jnp.einsum('btgp,ghp->btgh', si, c_im)


def s5_out(y, p):
    y = jax.nn.gelu(y)
    y = y * jax.nn.sigmoid(y @ p['ssm_w_glu'] + p['ssm_b_glu'])
    return y @ p['ssm_w_o']


def s5_branch(ul, uc, p, with_ctx):
    B, T, _ = ul.shape
    C = uc.shape[1]
    u = ul.astype(jnp.float32).reshape(B, T, SSM_GROUPS, SSM_GROUP)
    w = uc.astype(jnp.float32).reshape(B, C, SSM_GROUPS, SSM_GROUP)
    d_skip = p['ssm_d'].astype(jnp.float32).reshape(SSM_GROUPS, SSM_GROUP)
    c_re = p['ssm_c_re'].astype(jnp.float32)
    c_im = p['ssm_c_im'].astype(jnp.float32)
    y = d_skip * u
    yc = d_skip * w if with_ctx else None
    for d in range(2):
        reverse = d == 1
        ar, ai, bbr, bbi = s5_discretise(p['ssm_lam_re'][d], p['ssm_lam_im'][d], p['ssm_log_dt'][d], p['ssm_b_re'], p['ssm_b_im'])
        cr, ci = s5_scan(w, ar, ai, bbr, bbi, None, None, reverse)
        edge = 0 if reverse else C - 1
        sr, si = s5_scan(u, ar, ai, bbr, bbi, cr[:, edge], ci[:, edge], reverse)
        y = y + s5_readout(sr, si, c_re, c_im)
        if with_ctx:
            yc = yc + s5_readout(cr, ci, c_re, c_im)
    y = s5_out(y.reshape(B, T, SSM_WIDTH).astype(ul.dtype), p)
    if not with_ctx:
        return y, None
    return y, s5_out(yc.reshape(B, C, SSM_WIDTH).astype(uc.dtype), p)


def gla_chunked(q, k, v, log_a, s0):
    B, T, H, DK = k.shape
    DV = v.shape[-1]
    n = T // GLA_CHUNK

    def chunks(t):
        return t.reshape(B, n, GLA_CHUNK, H, t.shape[-1])

    k, v, log_a = chunks(k), chunks(v), chunks(log_a)
    b = jnp.cumsum(log_a, axis=2)
    b_last = b[:, :, -1]
    kv = jnp.einsum('bnlhd,bnlhv->bnhdv', k * jnp.exp(b_last[:, :, None] - b), v)

    def step(s, inp):
        dec, kv_c = inp
        return dec[..., None] * s + kv_c, s

    s_fin, s_prev = lax.scan(step, s0, (jnp.moveaxis(jnp.exp(b_last), 1, 0), jnp.moveaxis(kv, 1, 0)))
    if q is None:
        return None, s_fin
    q = chunks(q)
    s_prev = jnp.moveaxis(s_prev, 0, 1)
    q_in = q * jnp.exp(b)
    k_in = k * jnp.exp(-b)
    mask = jnp.tril(jnp.ones((GLA_CHUNK, GLA_CHUNK), dtype=bool))
    att = jnp.where(mask, jnp.einsum('bnlhd,bnmhd->bnhlm', q_in, k_in), 0.0)
    o = jnp.einsum('bnhlm,bnmhv->bnlhv', att, v) + jnp.einsum('bnlhd,bnhdv->bnlhv', q_in, s_prev)
    return o.reshape(B, T, H, DV), s_fin


def gla_out(o, r, p):
    B, T = o.shape[0], o.shape[1]
    o = o * lax.rsqrt(jnp.mean(o * o, axis=-1, keepdims=True) + EPS)
    o = o.reshape(B, T, GLA_WIDTH) * p['gla_norm'].astype(jnp.float32)
    return (o.astype(r.dtype) * jax.nn.silu(r)) @ p['gla_w_o']


def gla_branch(zl, zc, p, with_ctx):
    ql, kl, vl, rl, afl, abl = zl
    qc, kc, vc, rc, afc, abc = zc
    B = ql.shape[0]

    def heads(t):
        return t.astype(jnp.float32).reshape(t.shape[0], t.shape[1], GLA_HEADS, -1)

    def log_gate(a_lr, d):
        z = a_lr.astype(jnp.float32) @ p['gla_w_a2'][d].astype(jnp.float32) + p['gla_b_a2'][d].astype(jnp.float32)
        return heads(jax.nn.log_sigmoid(z) / GLA_GATE_TAU)

    def flip(t):
        return None if t is None else jnp.flip(t, axis=1)

    qscale = GLA_DK ** -0.5
    Ql, Kl, Vl = heads(ql) * qscale, heads(kl), heads(vl)
    Kc, Vc = heads(kc), heads(vc)
    Qc = heads(qc) * qscale if with_ctx else None
    s0 = jnp.zeros((B, GLA_HEADS, GLA_DK, GLA_DV), jnp.float32)
    oc_f, s_f = gla_chunked(Qc, Kc, Vc, log_gate(afc, 0), s0)
    o_f, _ = gla_chunked(Ql, Kl, Vl, log_gate(afl, 0), s_f)
    oc_b, s_b = gla_chunked(flip(Qc), flip(Kc), flip(Vc), flip(log_gate(abc, 1)), s0)
    o_b, _ = gla_chunked(flip(Ql), flip(Kl), flip(Vl), flip(log_gate(abl, 1)), s_b)
    y = gla_out(o_f + flip(o_b), rl, p)
    if not with_ctx:
        return y, None
    return y, gla_out(oc_f + flip(oc_b), rc, p)


def merge_branches(gates, ya, yb, yc, p):
    ga, gb, gc = jnp.split(jax.nn.sigmoid(gates), N_BRANCH, axis=-1)
    return (ga * ya + gb * yb + gc * yc) @ p['w_out']


def mixer(h, hc, p, cos, sin, with_ctx):
    z = split_cols(h @ p['w_in'], IN_SPLITS)
    zc = split_cols(hc @ p['w_in'], IN_SPLITS)
    ya, yac = mla_branch(z[0:3], zc[0:3], p, cos, sin, with_ctx)
    yb, ybc = s5_branch(z[3], zc[3], p, with_ctx)
    yg, ygc = gla_branch(z[4:10], zc[4:10], p, with_ctx)
    y = merge_branches(z[10], ya, yb, yg, p)
    if not with_ctx:
        return y, None
    return y, merge_branches(zc[10], yac, ybc, ygc, p)


def swiglu(h, p):
    g, u = jnp.split(h @ p['ffn_w_up'], 2, axis=-1)
    return (jax.nn.silu(g) * u) @ p['ffn_w_down']


def trunk_layer(x, xc, c, c_ctx, p, cos, sin, with_ctx):
    mod = jax.nn.silu(c) @ p['w_mod'] + p['b_mod']
    sh1, sc1, g1, sh2, sc2, g2 = jnp.split(mod[:, None, :], 6, axis=-1)
    modc = jax.nn.silu(c_ctx) @ p['w_mod'] + p['b_mod']
    shc1, scc1, gc1, shc2, scc2, gc2 = jnp.split(modc, 6, axis=-1)
    h = modulate(x, p['norm1'], sh1, sc1)
    hc = modulate(xc, p['norm1'], shc1, scc1)
    y, yc = mixer(h, hc, p, cos, sin, with_ctx)
    x = x + g1 * y
    x = x + g2 * swiglu(modulate(x, p['norm2'], sh2, sc2), p)
    if with_ctx:
        xc = xc + gc1 * yc
        xc = xc + gc2 * swiglu(modulate(xc, p['norm2'], shc2, scc2), p)
    return x, xc


def setup_inputs(seed: int = 0) -> dict:
    key = jax.random.key(seed)
    ks = iter(jax.random.split(key, 48))
    f32 = jnp.float32
    L, D, G, P, H16 = DEPTH, D_MODEL, SSM_GROUPS, SSM_STATE, SSM_GROUP

    def nrm(shape, scale):
        return scale * jax.random.normal(next(ks), shape, f32)

    def gain(shape):
        return 1.0 + 0.05 * jax.random.normal(next(ks), shape, f32)

    return {
        'x': nrm((BATCH, SEQ, D), 1.0),
        'c': nrm((BATCH, D), 1.0),
        'ctx': nrm((BATCH, CTX_LEN, D), 1.0),
        'c_ctx': nrm((D,), 1.0),
        'w_mod': nrm((L, D, 6 * D), 0.5 * D ** -0.5),
        'b_mod': nrm((L, 6 * D), 0.02),
        'norm1': gain((L, D)),
        'norm2': gain((L, D)),
        'w_in': nrm((L, D, IN_WIDTH), D ** -0.5),
        'mla_q_norm': gain((L, MLA_Q_RANK)),
        'mla_w_uq': nrm((L, MLA_Q_RANK, MLA_HEADS * (MLA_NOPE + MLA_ROPE)), MLA_Q_RANK ** -0.5),
        'mla_kv_norm': gain((L, MLA_KV_RANK)),
        'mla_w_ukv': nrm((L, MLA_KV_RANK, MLA_HEADS * (MLA_NOPE + MLA_V)), MLA_KV_RANK ** -0.5),
        'mla_w_o': nrm((L, MLA_WIDTH, D), MLA_WIDTH ** -0.5),
        'ssm_lam_re': -0.5 + 0.01 * jax.random.normal(next(ks), (L, 2, G, P), f32),
        'ssm_lam_im': math.pi * jnp.arange(P, dtype=f32) + 0.01 * jax.random.normal(next(ks), (L, 2, G, P), f32),
        'ssm_log_dt': jax.random.uniform(next(ks), (L, 2, G), f32, math.log(DT_MIN), math.log(DT_MAX)),
        'ssm_b_re': nrm((L, G, P, H16), (2 * H16) ** -0.5),
        'ssm_b_im': nrm((L, G, P, H16), (2 * H16) ** -0.5),
        'ssm_c_re': nrm((L, G, H16, P), P ** -0.5),
        'ssm_c_im': nrm((L, G, H16, P), P ** -0.5),
        'ssm_d': nrm((L, SSM_WIDTH), 0.5),
        'ssm_w_glu': nrm((L, SSM_WIDTH, SSM_WIDTH), SSM_WIDTH ** -0.5),
        'ssm_b_glu': nrm((L, SSM_WIDTH), 0.02),
        'ssm_w_o': nrm((L, SSM_WIDTH, D), SSM_WIDTH ** -0.5),
        'gla_w_a2': nrm((L, 2, GLA_GATE_RANK, GLA_HEADS * GLA_DK), 0.5 * GLA_GATE_RANK ** -0.5),
        'gla_b_a2': nrm((L, 2, GLA_HEADS * GLA_DK), 0.1),
        'gla_norm': gain((L, GLA_WIDTH)),
        'gla_w_o': nrm((L, GLA_WIDTH, D), GLA_WIDTH ** -0.5),
        'w_out': nrm((L, D, D), D ** -0.5),
        'ffn_w_up': nrm((L, D, 2 * FFN_HIDDEN), D ** -0.5),
        'ffn_w_down': nrm((L, FFN_HIDDEN, D), FFN_HIDDEN ** -0.5),
        'final_norm': gain((D,)),
    }


def reference(x, c, ctx, c_ctx, w_mod, b_mod, norm1, norm2, w_in, mla_q_norm, mla_w_uq, mla_kv_norm, mla_w_ukv, mla_w_o, ssm_lam_re, ssm_lam_im, ssm_log_dt, ssm_b_re, ssm_b_im, ssm_c_re, ssm_c_im, ssm_d, ssm_w_glu, ssm_b_glu, ssm_w_o, gla_w_a2, gla_b_a2, gla_norm, gla_w_o, w_out, ffn_w_up, ffn_w_down, final_norm):
    ROWS = x.shape[1] // GRID_W
    cos, sin = axial_angles(ROWS)
    xc = ctx
    for l in range(DEPTH):
        p = dict(w_mod=w_mod[l], b_mod=b_mod[l], norm1=norm1[l], norm2=norm2[l], w_in=w_in[l],
                 mla_q_norm=mla_q_norm[l], mla_w_uq=mla_w_uq[l], mla_kv_norm=mla_kv_norm[l],
                 mla_w_ukv=mla_w_ukv[l], mla_w_o=mla_w_o[l],
                 ssm_lam_re=ssm_lam_re[l], ssm_lam_im=ssm_lam_im[l], ssm_log_dt=ssm_log_dt[l],
                 ssm_b_re=ssm_b_re[l], ssm_b_im=ssm_b_im[l], ssm_c_re=ssm_c_re[l], ssm_c_im=ssm_c_im[l],
                 ssm_d=ssm_d[l], ssm_w_glu=ssm_w_glu[l], ssm_b_glu=ssm_b_glu[l], ssm_w_o=ssm_w_o[l],
                 gla_w_a2=gla_w_a2[l], gla_b_a2=gla_b_a2[l], gla_norm=gla_norm[l], gla_w_o=gla_w_o[l],
                 w_out=w_out[l], ffn_w_up=ffn_w_up[l], ffn_w_down=ffn_w_down[l])
        x, xc = trunk_layer(x, xc, c, c_ctx, p, cos, sin, l < DEPTH - 1)
    return rms_norm(x, final_norm)
```

```python
from concourse.bass_utils import run_bass_kernel_spmd
import numpy as np
import concourse.bass as bass
import concourse.mybir as mybir

F32 = mybir.dt.float32
BF16 = mybir.dt.bfloat16
ALU = mybir.AluOpType
AF = mybir.ActivationFunctionType
AX = mybir.AxisListType

NDSEM = 8


class Tile:
    __slots__ = ("name", "w", "r")

    def __init__(self, name):
        self.name = name
        self.w = None
        self.r = []


class Op:
    __slots__ = ("st", "dma", "fn", "idx", "didx", "waits", "dwaits", "sig", "cnt")


class K:
    STREAMS = ("pe", "act", "dve", "pool", "sp")

    def __init__(self, nc):
        self.nc = nc
        self.ops = {s: [] for s in self.STREAMS}
        self.ndma = {s: 0 for s in self.STREAMS}
        self.dmaops = {s: [] for s in self.STREAMS}
        self.tiles = {}
        self.sb_off = 16640
        self.sb_mark = 16640
        self.nalloc = 0
        self.psum = []
        self.newbufs = []
        self._inzero = False

    def t(self, *key):
        tl = self.tiles.get(key)
        if tl is None:
            tl = self.tiles[key] = Tile(key)
        return tl

    def sb(self, shape, dtype, name=None):
        self.nalloc += 1
        name = (name or "t") + "_%d" % self.nalloc
        esz = 4 if dtype == F32 else 2
        n = int(np.prod(shape[1:])) * esz
        n = (n + 63) // 64 * 64
        off = self.sb_off
        self.sb_off += n
        assert self.sb_off <= 229000, ("sbuf overflow", self.sb_off)
        h = self.nc.alloc_sbuf_tensor_at(name, list(shape), dtype, offset=off)
        self.newbufs.append(h)
        return h

    def zero_new(self):
        engs = ("pool", "dve")
        for i, h in enumerate(self.newbufs):
            self.op(engs[i % 2], lambda e, h=h: e.memset(h[:], 0.0), (), ())
        self.newbufs = []
        self.barrier()

    def mark(self):
        self.sb_mark = self.sb_off

    def reset(self):
        self.sb_off = self.sb_mark

    def _rec(self, st, dma, fn, reads, writes):
        if self.newbufs and not self._inzero:
            self._inzero = True
            self.zero_new()
            self._inzero = False
        op = Op()
        op.st, op.dma, op.fn = st, dma, fn
        op.idx = len(self.ops[st])
        op.sig = False
        op.cnt = 0
        op.didx = -1
        deps = set()
        for tl in reads:
            if tl.w is not None:
                deps.add(tl.w)
        for tl in writes:
            if tl.w is not None:
                deps.add(tl.w)
            deps.update(tl.r)
        for tl in reads:
            tl.r.append(op)
        for tl in writes:
            tl.w = op
            tl.r = []
        if dma:
            op.didx = self.ndma[st]
            self.ndma[st] += 1
            self.dmaops[st].append(op)
            if op.didx >= NDSEM:
                deps.add(self.dmaops[st][op.didx - NDSEM])
        op.waits = {}
        op.dwaits = set()
        for d in deps:
            if d is op:
                continue
            if d.dma:
                op.dwaits.add(d)
            else:
                if d.st == "pe" and st == "pe" and not dma:
                    continue
                if op.waits.get(d.st, -1) < d.idx:
                    op.waits[d.st] = d.idx
        self.ops[st].append(op)
        return op

    def op(self, st, fn, reads=(), writes=()):
        return self._rec(st, False, fn, reads, writes)

    def dma(self, st, out, in_, reads=(), writes=(), **kw):
        return self._rec(st, True, lambda e: e.dma_start(out=out, in_=in_, **kw), reads, writes)

    def barrier(self):
        bt = self.t("__barrier__", len(self.tiles))
        lasts = []
        for s in self.STREAMS:
            if self.ops[s]:
                for o in reversed(self.ops[s]):
                    if not o.dma:
                        lasts.append(o)
                        break
                lasts.extend(self.dmaops[s][-NDSEM:])
        self._blasts = lasts
        for s in ("pe", "act", "dve", "pool", "sp"):
            op = self._rec(s, False, lambda e: e.nop(), (), ())
            for d in lasts:
                if d.dma:
                    op.dwaits.add(d)
                elif not (d.st == "pe" and s == "pe"):
                    if op.waits.get(d.st, -1) < d.idx:
                        op.waits[d.st] = d.idx

    def emit(self):
        nc = self.nc
        for s in self.STREAMS:
            for o in self.ops[s]:
                for ps, pidx in o.waits.items():
                    self.ops[ps][pidx].sig = True
        for s in self.STREAMS:
            c = 0
            for o in self.ops[s]:
                if (not o.dma) and o.sig:
                    c += 1
                    o.cnt = c
        import contextlib, os as _os
        if _os.environ.get("KSTATS"):
            for s_ in self.STREAMS:
                print("KSTATS", s_, "ops", len(self.ops[s_]), "signals", max([o.cnt for o in self.ops[s_]] + [0]), "dmas", self.ndma[s_], flush=True)
        with contextlib.ExitStack() as es:
            csem = {s: es.enter_context(nc.semaphore("c_" + s)) for s in self.STREAMS}
            dsem = {s: [es.enter_context(nc.semaphore("d_%s%d" % (s, i))) for i in range(NDSEM)]
                    for s in self.STREAMS if self.ndma[s]}
            block = es.enter_context(nc.Block())
            K_ = self

            def run(stream, eng):
                known = {s: 0 for s in K_.STREAMS}
                dknown = set()
                for o in K_.ops[stream]:
                    for ps, pidx in o.waits.items():
                        need = K_.ops[ps][pidx].cnt
                        if known[ps] < need:
                            eng.wait_ge(csem[ps], need)
                            known[ps] = need
                    for d in o.dwaits:
                        key = (d.st, d.didx)
                        if key in dknown:
                            continue
                        eng.wait_ge(dsem[d.st][d.didx % NDSEM], 16 * (d.didx // NDSEM + 1))
                        dknown.add(key)
                    ins = o.fn(eng)
                    if o.dma:
                        ins.then_inc(dsem[stream][o.didx % NDSEM], 16)
                    elif o.sig:
                        ins.then_inc(csem[stream], 1)

            @block.tensor
            def _(e):
                run("pe", e)

            @block.scalar
            def _(e):
                run("act", e)

            @block.vector
            def _(e):
                run("dve", e)

            @block.gpsimd
            def _(e):
                run("pool", e)

            @block.sync
            def _(e):
                run("sp", e)

import math

D = 1024
CTX = 256
NFM = 42
SCALE = 96 ** -0.5
EPS = 1e-6


class Ctx:
    pass


def _mm(k, out, lhsT, rhs, start, stop, R, W):
    k.op("pe", lambda e, o=out, l=lhsT, r=rhs, s=start, p=stop: e.matmul(o, l, r, start=s, stop=p), R, W)


def _tt(k, eng, out, a, b, op, R, W):
    k.op(eng, lambda e, o=out, a=a, b=b, op=op: e.tensor_tensor(o, a, b, op), R, W)


def _act(k, out, in_, func, R, W, bias=None, scale=None):
    kw = {}
    if bias is not None:
        kw["bias"] = bias
    if scale is not None:
        kw["scale"] = scale
    k.op("act", lambda e, o=out, i=in_, f=func, kw=kw: e.activation(o, i, f, **kw), R, W)


def _copy(k, eng, out, in_, R, W):
    if eng == "act":
        k.op("act", lambda e, o=out, i=in_: e.copy(o, i), R, W)
    else:
        k.op(eng, lambda e, o=out, i=in_: e.tensor_copy(o, i), R, W)


def build_program(T, NL, stop=None, dbg=()):
    NT = CTX + T
    nc = bass.Bass("TRN2", target_bir_lowering=False)
    k = K(nc)
    TT = k.t
    C = Ctx()
    C.nc, C.k, C.T, C.NT, C.NL = nc, k, T, NT, NL
    C.NTILE = NT // 128

    def din(name, shape, dt=F32):
        return nc.dram_tensor(name, list(shape), dt, kind="ExternalInput").ap()

    def dscr(name, shape, dt):
        return nc.dram_tensor(name, list(shape), dt, kind=("ExternalOutput" if name in dbg else "Internal")).ap()

    I = {}
    I["x"] = din("x", [T, D]); I["ctx"] = din("ctx", [CTX, D])
    I["cc"] = din("cc", [2, D])
    I["w_mod"] = din("w_mod", [NL, D, 6 * D]); I["b_mod"] = din("b_mod", [NL, 6 * D])
    I["norm1"] = din("norm1", [NL, D]); I["norm2"] = din("norm2", [NL, D])
    I["w_in"] = din("w_in", [NL, D, NFM * 128]); I["w_tm"] = din("w_tm", [NL, D, 768])
    I["qn"] = din("qn", [NL, 256]); I["kvn"] = din("kvn", [NL, 256])
    I["w_uq"] = din("w_uq", [NL, 256, 768]); I["w_uqs"] = din("w_uqs", [NL, 256, 768])
    I["w_ukv"] = din("w_ukv", [NL, 256, 1024]); I["w_o"] = din("w_o", [NL, 512, D])
    I["lam_re"] = din("lam_re", [NL, 2, 32, 64]); I["lam_im"] = din("lam_im", [NL, 2, 32, 64])
    I["log_dt"] = din("log_dt", [NL, 2, 32])
    I["b_re"] = din("b_re", [NL, 32, 64, 16]); I["b_im"] = din("b_im", [NL, 32, 64, 16])
    I["c_re"] = din("c_re", [NL, 32, 16, 64]); I["c_im"] = din("c_im", [NL, 32, 16, 64])
    I["ssm_d"] = din("ssm_d", [NL, 512]); I["w_glu"] = din("w_glu", [NL, 512, 512])
    I["b_glu"] = din("b_glu", [NL, 512]); I["s_wo"] = din("s_wo", [NL, 512, D])
    I["w_a2"] = din("w_a2", [NL, 2, 16, 256]); I["b_a2"] = din("b_a2", [NL, 2, 256])
    I["g_norm"] = din("g_norm", [NL, 512]); I["g_wo"] = din("g_wo", [NL, 512, D])
    I["w_out"] = din("w_out", [NL, D, D]); I["w_up"] = din("w_up", [NL, D, 5632])
    I["w_dn"] = din("w_dn", [NL, 2816, D]); I["fnorm"] = din("fnorm", [D])
    I["cosR"] = din("cosR", [32, T]); I["sinS"] = din("sinS", [32, T])
    I["cst"] = din("cst", [128, 1024])
    I["mk"] = din("mk", [128, 8])
    out = nc.dram_tensor("out", [T, D], F32, kind="ExternalOutput").ap()
    C.I, C.out = I, out

    S = {}
    S["xT"] = dscr("xT", [D, NT], F32)
    S["zT"] = dscr("zT", [NFM * 128, NT], BF16)
    S["QT"] = dscr("QT", [8, 96, NT], BF16); S["KT"] = dscr("KT", [8, 96, NT], BF16)
    S["V"] = dscr("Vs", [NT, 512], BF16)
    S["gkv"] = dscr("gkv", [NT, 768], BF16)
    S["attO"] = dscr("attO", [512, NT], BF16)
    S["yf"] = dscr("yf", [512, NT], F32); S["yb"] = dscr("yb", [512, NT], F32)
    S["s5A"] = dscr("s5A", [512, NT], BF16)
    S["of"] = dscr("of", [512, NT], F32); S["glaA"] = dscr("glaA", [512, NT], BF16)
    C.S = S

    C.ps = [nc.alloc_psum_tensor("ps%d" % i, [128, 512], F32) for i in range(8)]
    C.cst = k.sb([128, 1024], F32)
    C.cstb = k.sb([128, 1024], BF16)
    C.mk = k.sb([128, 8], F32)
    C.modT = k.sb([128, 48, 2], F32)
    C.vec = k.sb([128, 6, 8, 2], F32)
    C.epsT = k.sb([128, 1], F32)
    C.oneT = k.sb([128, 1], F32)
    C.hpiT = k.sb([128, 1], F32)
    C.onesb = k.sb([128, 128], BF16)
    C.onesf = k.sb([128, 128], F32)
    k.dma("sp", C.cst[:], I["cst"], writes=[TT("cst")])
    k.dma("pool", C.cstb[:], I["cst"], writes=[TT("cstb")])
    k.dma("sp", C.mk[:], I["mk"], writes=[TT("mk")])
    k.op("dve", lambda e: e.memset(C.epsT[:], EPS), [], [TT("eps")])
    k.op("dve", lambda e: e.memset(C.oneT[:], 1.0), [], [TT("one")])
    k.op("dve", lambda e: e.memset(C.hpiT[:], math.pi / 2), [], [TT("hpi")])
    k.op("dve", lambda e: e.memset(C.onesb[:], 1.0), [], [TT("onesb")])
    k.op("dve", lambda e: e.memset(C.onesf[:], 1.0), [], [TT("onesf")])
    k.mark()

    C.mts = [(0, CTX, True)] + [(CTX + i * 512, 512, False) for i in range(T // 512)]

    def _phases():
        yield "load", lambda: phase_load_x(C)
        for l in range(NL):
            last = (l == NL - 1)
            yield "mod%d" % l, lambda: phase_mod(C, l)
            yield "A%d" % l, lambda: phase_A(C, l)
            yield "qkv%d" % l, lambda: phase_qkv(C, l)
            yield "attn%d" % l, lambda: phase_attn(C, l)
            yield "s5%d" % l, lambda: phase_s5(C, l)
            yield "gla%d" % l, lambda: phase_gla(C, l)
            yield "E1%d" % l, lambda: phase_E1(C, l, last)
            yield "E2%d" % l, lambda: phase_E2(C, l, last)

    for name, fn in _phases():
        fn()
        if stop == name:
            break
    if "vec" in dbg:
        dv = nc.dram_tensor("vec", [128, 96], F32, kind="ExternalOutput").ap()
        k.dma("sp", dv, C.vec[:].rearrange("p a b c -> p (a b c)"), reads=[TT("vec")])
    k.barrier()
    k.emit()
    return nc


def mt_of_tile(ti):
    return 0 if ti < 2 else 1 + (ti - 2) // 4


def phase_load_x(C):
    k, nc, I, S = C.k, C.nc, C.I, C.S
    TT = k.t
    ident = C.cst[:, 0:128]
    xin = [k.sb([128, D], F32) for _ in range(2)]
    xo = [k.sb([128, 8, 128], F32) for _ in range(2)]
    for ti in range(C.NTILE):
        src = I["ctx"][ti * 128:(ti + 1) * 128, :] if ti < 2 else I["x"][(ti - 2) * 128:(ti - 1) * 128, :]
        b = ti % 2
        k.dma("sp", xin[b][:], src, writes=[TT("xin", b)])
        for kt in range(8):
            pb = kt
            k.op("pe", lambda e, o=C.ps[pb][:, 0:128], i=xin[b][:, kt * 128:(kt + 1) * 128]: e.transpose(o, i, ident),
                 [TT("xin", b), TT("cst")], [TT("ps", pb)])
            _copy(k, "act" if kt % 2 else "dve", xo[b][:, kt, :], C.ps[pb][:, 0:128], [TT("ps", pb)], [TT("xo", b)])
        k.dma("pool", S["xT"].rearrange("(kt p) t -> p kt t", p=128)[:, :, ti * 128:(ti + 1) * 128], xo[b][:],
              reads=[TT("xo", b)], writes=[TT("xT", mt_of_tile(ti))])
    k.barrier()
    k.reset()


def phase_mod(C, l):
    k, nc, I = C.k, C.nc, C.I
    TT = k.t
    cin = k.sb([128, 8, 2], F32)
    sc = k.sb([128, 8, 2], F32)
    for j in range(2):
        k.dma("sp", cin[:, :, j], I["cc"][j].rearrange("(kt p) -> p kt", p=128), writes=[TT("cin")], allow_slow_non_contiguous=True)
    _act(k, sc[:], cin[:], AF.Silu, [TT("cin")], [TT("sc")])
    bm = k.sb([128, 48], F32)
    k.dma("sp", bm[:], I["b_mod"][l].rearrange("(ft p) -> p ft", p=128), writes=[TT("bm")], allow_slow_non_contiguous=True)
    n12 = k.sb([128, 2, 8], F32)
    k.dma("sp", n12[:, 0, :], I["norm1"][l].rearrange("(ft p) -> p ft", p=128), writes=[TT("n12")], allow_slow_non_contiguous=True)
    k.dma("sp", n12[:, 1, :], I["norm2"][l].rearrange("(ft p) -> p ft", p=128), writes=[TT("n12")], allow_slow_non_contiguous=True)
    wm = [k.sb([128, 8, 1024], F32) for _ in range(2)]
    for ch in range(6):
        b = ch % 2
        k.dma("sp", wm[b][:], I["w_mod"][l].rearrange("(kt p) c -> p kt c", p=128)[:, :, ch * 1024:(ch + 1) * 1024],
              writes=[TT("wm", b)])
        for f in range(8):
            ft = ch * 8 + f
            pb = ft % 8
            for kt in range(8):
                _mm(k, C.ps[pb][:, 0:2], wm[b][:, kt, f * 128:(f + 1) * 128], sc[:, kt, :], kt == 0, kt == 7,
                    [TT("wm", b), TT("sc")], [TT("ps", pb)])
            k.op("dve", lambda e, o=C.modT[:, ft, :], i=C.ps[pb][:, 0:2], s=bm[:, ft:ft + 1]: e.tensor_scalar(o, i, s, None, ALU.add),
                 [TT("ps", pb), TT("bm")], [TT("modT")])
    m = C.modT
    for half in range(2):
        o0 = half * 24
        k.op("dve", lambda e, o=C.vec[:, 3 * half + 0, :, :], i=m[:, o0 + 8:o0 + 16, :], n=n12[:, half, :]:
             e.scalar_tensor_tensor(o, i, 1.0, n.unsqueeze(2).to_broadcast([128, 8, 2]), ALU.add, ALU.mult),
             [TT("modT"), TT("n12")], [TT("vec")])
        _copy(k, "dve", C.vec[:, 3 * half + 1, :, :], m[:, o0:o0 + 8, :], [TT("modT")], [TT("vec")])
        _copy(k, "dve", C.vec[:, 3 * half + 2, :, :], m[:, o0 + 16:o0 + 24, :], [TT("modT")], [TT("vec")])
    k.barrier()
    k.reset()


def colsum_rstd(C, sq_list, n, rs, denom, R, rs_tag, pb=7):
    k = C.k
    TT = k.t
    for i, sq in enumerate(sq_list):
        _mm(k, C.ps[pb][:, 0:n], C.onesb[:], sq, i == 0, i == len(sq_list) - 1, R + [TT("onesb")], [TT("ps", pb)])
    _act(k, rs[:, 0:n], C.ps[pb][:, 0:n], AF.Sqrt, [TT("ps", pb), TT("eps")], [rs_tag], bias=C.epsT[:, 0:1], scale=1.0 / denom)
    k.op("dve", lambda e: e.reciprocal(rs[:, 0:n], rs[:, 0:n]), [rs_tag], [rs_tag])


def norm_mod(C, xt, xtag, n, which, col, hT, sq, rs, tmp):
    k = C.k
    TT = k.t
    _act(k, sq[:, :, 0:n], xt[:, :, 0:n], AF.Square, [xtag], [TT("nm_sq")])
    colsum_rstd(C, [sq[:, kt, 0:n] for kt in range(8)], n, rs, float(D), [TT("nm_sq")], TT("nm_rs"))
    for kt in range(8):
        t = tmp[kt % 2]
        _tt(k, "dve", t[:, 0:n], xt[:, kt, 0:n], rs[:, 0:n], ALU.mult, [xtag, TT("nm_rs")], [TT("nm_tmp", kt % 2)])
        _act(k, hT[:, kt, 0:n], t[:, 0:n], AF.Identity, [TT("nm_tmp", kt % 2), TT("vec")], [TT("hT")],
             bias=C.vec[:, which + 1, kt, col:col + 1], scale=C.vec[:, which, kt, col:col + 1])


def load_w_bf(C, dst, src, tag, nsplit=1):
    k = C.k
    a = dst.shape[1]
    step = max(1, a // nsplit)
    for i in range(0, a, step):
        k.dma("pool", dst[:, i:i + step, :], src[:, i:i + step, :], writes=[tag])


def phase_A(C, l):
    k, nc, I, S = C.k, C.nc, C.I, C.S
    TT = k.t
    win = k.sb([128, 8, NFM * 128], BF16)
    wtm = k.sb([128, 8, 768], BF16)
    load_w_bf(C, win[:], I["w_in"][l].rearrange("(kt p) c -> p kt c", p=128), TT("win"), 8)
    load_w_bf(C, wtm[:], I["w_tm"][l].rearrange("(kt p) c -> p kt c", p=128), TT("wtm"), 2)
    xt = [k.sb([128, 8, 512], F32) for _ in range(2)]
    sq = k.sb([128, 8, 512], BF16)
    rs = k.sb([128, 512], F32)
    tmp = [k.sb([128, 512], F32) for _ in range(2)]
    hT = k.sb([128, 8, 512], BF16)
    stg = [k.sb([128, 512], BF16) for _ in range(6)]
    stm = [k.sb([128, 768], BF16) for _ in range(2)]
    nst = 0
    xTv = S["xT"].rearrange("(kt p) t -> p kt t", p=128)
    for mi, (c0, n, isctx) in enumerate(C.mts):
        b = mi % 2
        k.dma("sp", xt[b][:, :, 0:n], xTv[:, :, c0:c0 + n], reads=[TT("xT", mi)], writes=[TT("xtA", b)])
        norm_mod(C, xt[b], TT("xtA", b), n, 0, 1 if isctx else 0, hT, sq, rs, tmp)
        for ct in range(NFM):
            pb = ct % 6
            for kt in range(8):
                _mm(k, C.ps[pb][:, 0:n], win[:, kt, ct * 128:(ct + 1) * 128], hT[:, kt, 0:n], kt == 0, kt == 7,
                    [TT("win"), TT("hT")], [TT("ps", pb)])
            sb_ = nst % 6
            nst += 1
            if ct >= 18:
                _act(k, stg[sb_][:, 0:n], C.ps[pb][:, 0:n], AF.Sigmoid, [TT("ps", pb)], [TT("stg", sb_)])
            elif ct >= 14:
                _act(k, stg[sb_][:, 0:n], C.ps[pb][:, 0:n], AF.Silu, [TT("ps", pb)], [TT("stg", sb_)])
            else:
                _copy(k, "dve", stg[sb_][:, 0:n], C.ps[pb][:, 0:n], [TT("ps", pb)], [TT("stg", sb_)])
            k.dma("pool", S["zT"][ct * 128:(ct + 1) * 128, c0:c0 + n], stg[sb_][:, 0:n],
                  reads=[TT("stg", sb_)], writes=[TT("zT", ct, mi)])
        for j in range(n // 128):
            ti = c0 // 128 + j
            _ps = (6, 7)
            for kt in range(8):
                _mm(k, C.ps[6][:, 0:256], hT[:, kt, j * 128:(j + 1) * 128], wtm[:, kt, 0:256], kt == 0, kt == 7,
                    [TT("wtm"), TT("hT")], [TT("ps", 6)])
            for kt in range(8):
                _mm(k, C.ps[7][:, 0:512], hT[:, kt, j * 128:(j + 1) * 128], wtm[:, kt, 256:768], kt == 0, kt == 7,
                    [TT("wtm"), TT("hT")], [TT("ps", 7)])
            sb_ = ti % 2
            _copy(k, "dve", stm[sb_][:, 0:256], C.ps[6][:, 0:256], [TT("ps", 6)], [TT("stm", sb_)])
            _copy(k, "act", stm[sb_][:, 256:768], C.ps[7][:, 0:512], [TT("ps", 7)], [TT("stm", sb_)])
            k.dma("pool", S["gkv"][ti * 128:(ti + 1) * 128, :], stm[sb_][:], reads=[TT("stm", sb_)], writes=[TT("gkv", ti)])
    k.barrier()
    k.reset()


def phase_qkv(C, l):
    k, nc, I, S = C.k, C.nc, C.I, C.S
    TT = k.t
    wuq = k.sb([128, 2, 768], BF16); wuqs = k.sb([128, 2, 768], BF16); wukv = k.sb([128, 2, 1024], BF16)
    load_w_bf(C, wuq[:], I["w_uq"][l].rearrange("(kt p) c -> p kt c", p=128), TT("wuq"))
    load_w_bf(C, wuqs[:], I["w_uqs"][l].rearrange("(kt p) c -> p kt c", p=128), TT("wuqs"))
    load_w_bf(C, wukv[:], I["w_ukv"][l].rearrange("(kt p) c -> p kt c", p=128), TT("wukv"))
    gn = k.sb([128, 2, 2], F32)
    k.dma("sp", gn[:, 0, :], I["qn"][l].rearrange("(kt p) -> p kt", p=128), writes=[TT("gn")], allow_slow_non_contiguous=True)
    k.dma("sp", gn[:, 1, :], I["kvn"][l].rearrange("(kt p) -> p kt", p=128), writes=[TT("gn")], allow_slow_non_contiguous=True)
    zq = k.sb([128, 2, 512], BF16); zkv = k.sb([128, 2, 512], BF16)
    m1 = k.sb([128, 512], BF16); m2 = k.sb([128, 512], BF16)
    sqq = k.sb([128, 2, 512], BF16); sqk = k.sb([128, 2, 512], BF16)
    cqg = k.sb([128, 2, 512], BF16); ckg = k.sb([128, 2, 512], BF16)
    rq = k.sb([128, 512], F32); rk = k.sb([128, 512], F32)
    cosT = k.sb([128, 512], F32); sinT = k.sb([128, 512], F32)
    cr = k.sb([128, 512], F32); sr = k.sb([128, 512], F32)
    t1 = k.sb([128, 512], F32); t2 = k.sb([128, 512], F32); krr = k.sb([128, 512], F32)
    QTt = k.sb([128, 8, 512], BF16); KTt = k.sb([128, 8, 512], BF16)
    rkt = k.sb([128, 4], F32)
    Vt = [k.sb([128, 512], BF16) for _ in range(2)]
    zTv = S["zT"]
    R6 = slice(64, 96)
    for mi, (c0, n, isctx) in enumerate(C.mts):
        k.dma("sp", zq[:, :, 0:n], zTv[0:256, c0:c0 + n].rearrange("(kt p) t -> p kt t", p=128),
              reads=[TT("zT", 0, mi), TT("zT", 1, mi)], writes=[TT("zq")])
        k.dma("sp", zkv[:, :, 0:n], zTv[256:512, c0:c0 + n].rearrange("(kt p) t -> p kt t", p=128),
              reads=[TT("zT", 2, mi), TT("zT", 3, mi)], writes=[TT("zkv")])
        k.dma("sp", m1[:, 0:n], zTv[512:640, c0:c0 + n], reads=[TT("zT", 4, mi)], writes=[TT("m1")])
        k.dma("sp", m2[:, 0:n], zTv[640:768, c0:c0 + n], reads=[TT("zT", 5, mi)], writes=[TT("m2")])
        if not isctx:
            k.dma("sp", cosT[R6, 0:n], I["cosR"][:, c0 - CTX:c0 - CTX + n], writes=[TT("cosT")])
            k.dma("sp", sinT[R6, 0:n], I["sinS"][:, c0 - CTX:c0 - CTX + n], writes=[TT("sinT")])
        _act(k, sqq[:, :, 0:n], zq[:, :, 0:n], AF.Square, [TT("zq")], [TT("sqq")])
        _act(k, sqk[:, :, 0:n], zkv[:, :, 0:n], AF.Square, [TT("zkv")], [TT("sqk")])
        colsum_rstd(C, [sqq[:, kt, 0:n] for kt in range(2)], n, rq, 256.0, [TT("sqq")], TT("rq"), pb=6)
        colsum_rstd(C, [sqk[:, kt, 0:n] for kt in range(2)], n, rk, 256.0, [TT("sqk")], TT("rk"), pb=7)
        for kt in range(2):
            k.op("dve", lambda e, kt=kt, n=n: e.tensor_scalar(cqg[:, kt, 0:n], zq[:, kt, 0:n], gn[:, 0, kt:kt + 1], None, ALU.mult),
                 [TT("zq"), TT("gn")], [TT("cqg")])
            k.op("pool", lambda e, kt=kt, n=n: e.tensor_scalar(ckg[:, kt, 0:n], zkv[:, kt, 0:n], gn[:, 1, kt:kt + 1], None, ALU.mult),
                 [TT("zkv"), TT("gn")], [TT("ckg")])
        if not isctx:
            _tt(k, "dve", cr[R6, 0:n], cosT[R6, 0:n], rq[R6, 0:n], ALU.mult, [TT("cosT"), TT("rq")], [TT("cr")])
            _tt(k, "dve", sr[R6, 0:n], sinT[R6, 0:n], rq[R6, 0:n], ALU.mult, [TT("sinT"), TT("rq")], [TT("sr")])
        for h in range(8):
            pa, pb = (2 * h) % 6, (2 * h + 1) % 6
            for kt in range(2):
                _mm(k, C.ps[pa][0:96, 0:n], wuq[:, kt, h * 96:(h + 1) * 96], cqg[:, kt, 0:n], kt == 0, kt == 1,
                    [TT("wuq"), TT("cqg")], [TT("ps", pa)])
            _tt(k, "dve", QTt[0:64, h, 0:n], C.ps[pa][0:64, 0:n], rq[0:64, 0:n], ALU.mult, [TT("ps", pa), TT("rq")], [TT("QTt")])
            if isctx:
                _tt(k, "dve", QTt[R6, h, 0:n], C.ps[pa][R6, 0:n], rq[R6, 0:n], ALU.mult, [TT("ps", pa), TT("rq")], [TT("QTt")])
            else:
                for kt in range(2):
                    _mm(k, C.ps[pb][0:96, 0:n], wuqs[:, kt, h * 96:(h + 1) * 96], cqg[:, kt, 0:n], kt == 0, kt == 1,
                        [TT("wuqs"), TT("cqg")], [TT("ps", pb)])
                _tt(k, "dve", t1[R6, 0:n], C.ps[pa][R6, 0:n], cr[R6, 0:n], ALU.mult, [TT("ps", pa), TT("cr")], [TT("t1")])
                _tt(k, "dve", t2[R6, 0:n], C.ps[pb][R6, 0:n], sr[R6, 0:n], ALU.mult, [TT("ps", pb), TT("sr")], [TT("t2")])
                _tt(k, "pool", QTt[R6, h, 0:n], t1[R6, 0:n], t2[R6, 0:n], ALU.add, [TT("t1"), TT("t2")], [TT("QTt")])
        k.dma("pool", S["QT"][:, :, c0:c0 + n].rearrange("h r t -> r h t"), QTt[0:96, :, 0:n], reads=[TT("QTt")], writes=[TT("QT", mi)])
        for h in range(8):
            pa = h % 6
            for kt in range(2):
                _mm(k, C.ps[pa][0:64, 0:n], wukv[:, kt, h * 128:h * 128 + 64], ckg[:, kt, 0:n], kt == 0, kt == 1,
                    [TT("wukv"), TT("ckg")], [TT("ps", pa)])
            _tt(k, "dve", KTt[0:64, h, 0:n], C.ps[pa][0:64, 0:n], rk[0:64, 0:n], ALU.mult, [TT("ps", pa), TT("rk")], [TT("KTt")])
        if isctx:
            _copy(k, "dve", krr[R6, 0:n], m1[R6, 0:n], [TT("m1")], [TT("krr")])
        else:
            _tt(k, "dve", t1[R6, 0:n], m1[R6, 0:n], cosT[R6, 0:n], ALU.mult, [TT("m1"), TT("cosT")], [TT("t1")])
            _tt(k, "dve", t2[R6, 0:n], m2[R6, 0:n], sinT[R6, 0:n], ALU.mult, [TT("m2"), TT("sinT")], [TT("t2")])
            _tt(k, "pool", krr[R6, 0:n], t1[R6, 0:n], t2[R6, 0:n], ALU.add, [TT("t1"), TT("t2")], [TT("krr")])
        _copy(k, "pool", KTt[R6, :, 0:n], krr[R6, 0:n].unsqueeze(1).to_broadcast([32, 8, n]), [TT("krr")], [TT("KTt")])
        k.dma("pool", S["KT"][:, :, c0:c0 + n].rearrange("h r t -> r h t"), KTt[0:96, :, 0:n], reads=[TT("KTt")], writes=[TT("KT", mi)])
        wv = wukv[:].rearrange("p k (h c) -> p k h c", c=128)
        for j in range(n // 128):
            ti = c0 // 128 + j
            js = slice(j * 128, (j + 1) * 128)
            for kt in range(2):
                _mm(k, C.ps[6][:, 0:1], sqk[:, kt, js], C.onesb[:, 0:1], kt == 0, kt == 1, [TT("sqk"), TT("onesb")], [TT("ps", 6)])
            _act(k, rkt[:, j:j + 1], C.ps[6][:, 0:1], AF.Sqrt, [TT("ps", 6), TT("eps")], [TT("rkt")], bias=C.epsT[:, 0:1], scale=1.0 / 256)
            k.op("dve", lambda e, j=j: e.reciprocal(rkt[:, j:j + 1], rkt[:, j:j + 1]), [TT("rkt")], [TT("rkt")])
            pv = 7
            for kt in range(2):
                _mm(k, C.ps[pv][:, 0:512].rearrange("p (h c) -> p h c", c=64), ckg[:, kt, js], wv[:, kt, :, 64:128], kt == 0, kt == 1,
                    [TT("wukv"), TT("ckg")], [TT("ps", pv)])
            vb = ti % 2
            k.op("dve", lambda e, j=j, vb=vb: e.tensor_scalar(Vt[vb][:], C.ps[pv][:, 0:512], rkt[:, j:j + 1], None, ALU.mult),
                 [TT("ps", pv), TT("rkt")], [TT("Vt", vb)])
            k.dma("pool", S["V"][ti * 128:(ti + 1) * 128, :], Vt[vb][:], reads=[TT("Vt", vb)], writes=[TT("V", ti)])
    k.barrier()
    k.reset()


def phase_attn(C, l):
    k, nc, I, S = C.k, C.nc, C.I, C.S
    TT = k.t
    NT, NTILE = C.NT, C.NTILE
    KTh = [k.sb([128, NT], BF16) for _ in range(2)]
    QTh = [k.sb([128, NT], BF16) for _ in range(2)]
    Vh = [k.sb([128, NTILE, 65], BF16) for _ in range(2)]
    P = [k.sb([128, 512], BF16) for _ in range(4)]
    rden = k.sb([128, 512], F32)
    bc = k.sb([128, 512], F32)
    on = [k.sb([128, 512], BF16) for _ in range(2)]
    for b in range(2):
        k.op("dve", lambda e, b=b: e.memset(Vh[b][:, :, 64:65], 1.0), [], [TT("Vh", b)])
    allmt = list(range(len(C.mts)))
    items = []
    for h in range(8):
        for mi, (c0, n, isctx) in enumerate(C.mts):
            nkt = 2 if isctx else NTILE
            for kt in range(nkt):
                items.append((h, mi, kt, nkt))
    LA = 3
    qidx = {}
    for it_ in items:
        key = (it_[0], it_[1])
        if key not in qidx:
            qidx[key] = len(qidx)
    loaded = set()

    def issue_S(i):
        h, mi, kt, nkt = items[i]
        c0, n, isctx = C.mts[mi]
        b = h % 2
        if h not in loaded:
            loaded.add(h)
            k.dma("sp", KTh[b][0:96, :], S["KT"][h], reads=[TT("KT", m_) for m_ in allmt], writes=[TT("KTh", b)])
            k.dma("sp", QTh[b][0:96, :], S["QT"][h], reads=[TT("QT", m_) for m_ in allmt], writes=[TT("QTh", b)])
            k.dma("sp", Vh[b][:, :, 0:64], S["V"].rearrange("(n p) (h d) -> p n h d", p=128, d=64)[:, :, h, :],
                  reads=[TT("V", ti) for ti in range(NTILE)], writes=[TT("Vh", b)])
        pscr = i % 4
        _mm(k, C.ps[pscr][:, 0:n], KTh[b][0:96, kt * 128:(kt + 1) * 128], QTh[b][0:96, c0:c0 + n], True, True,
            [TT("KTh", b), TT("QTh", b)], [TT("ps", pscr)])
        _act(k, P[pscr][:, 0:n], C.ps[pscr][:, 0:n], AF.Exp, [TT("ps", pscr)], [TT("P", pscr)], scale=SCALE)

    def issue_PV(i):
        h, mi, kt, nkt = items[i]
        c0, n, isctx = C.mts[mi]
        b = h % 2
        pscr = i % 4
        nq = qidx[(h, mi)]
        po = 4 + nq % 2
        _mm(k, C.ps[po][0:65, 0:n], Vh[b][:, kt, :], P[pscr][:, 0:n], kt == 0, kt == nkt - 1,
            [TT("Vh", b), TT("P", pscr)], [TT("ps", po)])
        if kt == nkt - 1:
            k.op("dve", lambda e, po=po, n=n: e.reciprocal(rden[64:65, 0:n], C.ps[po][64:65, 0:n]), [TT("ps", po)], [TT("rden")])
            _mm(k, C.ps[6][0:64, 0:n], C.onesf[64:65, 0:64], rden[64:65, 0:n], True, True, [TT("rden"), TT("onesf")], [TT("ps", 6)])
            _copy(k, "act", bc[0:64, 0:n], C.ps[6][0:64, 0:n], [TT("ps", 6)], [TT("bc")])
            ob = nq % 2
            _tt(k, "dve", on[ob][0:64, 0:n], C.ps[po][0:64, 0:n], bc[0:64, 0:n], ALU.mult, [TT("ps", po), TT("bc")], [TT("on", ob)])
            k.dma("pool", S["attO"][h * 64:(h + 1) * 64, c0:c0 + n], on[ob][0:64, 0:n], reads=[TT("on", ob)], writes=[TT("attO", mi)])

    for i in range(len(items) + LA):
        if i < len(items):
            issue_S(i)
        if i >= LA:
            issue_PV(i - LA)
    k.barrier()
    k.reset()


def phase_s5(C, l):
    k, nc, I, S = C.k, C.nc, C.I, C.S
    TT = k.t
    NTILE = C.NTILE
    P_ = [TT("s5p")]

    def ew(out, a, b, op, eng="dve"):
        _tt(k, eng, out, a, b, op, P_, P_)

    def ts(out, a, s1, s2, op0, op1=None):
        if op1 is None:
            k.op("dve", lambda e: e.tensor_scalar(out, a, s1, None, op0), P_, P_)
        else:
            k.op("dve", lambda e: e.tensor_scalar(out, a, s1, s2, op0, op1), P_, P_)

    def disc(lr, li, ld, shp):
        names = ["dt", "a", "mag", "ang", "s", "c", "t1", "t2", "ar", "ai", "den", "am1", "fr", "fi"]
        t = {nm: k.sb([128] + shp, F32) for nm in names}
        A = lambda x: x[:]
        _act(k, A(t["dt"]), A(ld), AF.Exp, P_, P_)
        ew(A(t["a"]), A(t["dt"]), A(lr), ALU.mult)
        _act(k, A(t["mag"]), A(t["a"]), AF.Exp, P_, P_)
        ew(A(t["ang"]), A(t["dt"]), A(li), ALU.mult)
        _act(k, A(t["s"]), A(t["ang"]), AF.Sin, P_, P_, scale=1.0 / 16)
        _act(k, A(t["c"]), A(t["ang"]), AF.Sin, P_ + [TT("hpi")], P_, bias=C.hpiT[:, 0:1], scale=1.0 / 16)
        for _ in range(4):
            ew(A(t["t1"]), A(t["c"]), A(t["c"]), ALU.mult)
            ew(A(t["t2"]), A(t["s"]), A(t["s"]), ALU.mult)
            ew(A(t["s"]), A(t["s"]), A(t["c"]), ALU.mult)
            ts(A(t["s"]), A(t["s"]), 2.0, None, ALU.mult)
            ew(A(t["c"]), A(t["t1"]), A(t["t2"]), ALU.subtract)
        ew(A(t["ar"]), A(t["mag"]), A(t["c"]), ALU.mult)
        ew(A(t["ai"]), A(t["mag"]), A(t["s"]), ALU.mult)
        ew(A(t["t1"]), A(lr), A(lr), ALU.mult)
        ew(A(t["t2"]), A(li), A(li), ALU.mult)
        ew(A(t["den"]), A(t["t1"]), A(t["t2"]), ALU.add)
        k.op("dve", lambda e: e.reciprocal(A(t["den"]), A(t["den"])), P_, P_)
        ts(A(t["am1"]), A(t["ar"]), -1.0, None, ALU.add)
        ew(A(t["t1"]), A(t["am1"]), A(lr), ALU.mult)
        ew(A(t["t2"]), A(t["ai"]), A(li), ALU.mult)
        ew(A(t["t1"]), A(t["t1"]), A(t["t2"]), ALU.add)
        ew(A(t["fr"]), A(t["t1"]), A(t["den"]), ALU.mult)
        ew(A(t["t1"]), A(t["ai"]), A(lr), ALU.mult)
        ew(A(t["t2"]), A(t["am1"]), A(li), ALU.mult)
        ew(A(t["t1"]), A(t["t1"]), A(t["t2"]), ALU.subtract)
        ew(A(t["fi"]), A(t["t1"]), A(t["den"]), ALU.mult)
        return t

    CS = k.sb([128, 2, 2, 16, 128], F32)
    RT = k.sb([128, 32, 128], F32)
    Ctab = CS[:, :, 0]
    Stab = CS[:, :, 1]
    XL = k.sb([128, 2, 2, 4, 4, 2, 64], BF16)
    RL = k.sb([128, 2, 16, 128], BF16)
    dsk = k.sb([128, 4], F32)
    k.dma("sp", dsk[:], I["ssm_d"][l].rearrange("(ft p) -> p ft", p=128), writes=[TT("dsk")], allow_slow_non_contiguous=True)
    mark2 = k.sb_off
    lrA = k.sb([128, 32], F32); liA = k.sb([128, 32], F32); ldA = k.sb([128, 32], F32)
    for d_ in range(2):
        k.dma("sp", lrA[:, d_ * 16:(d_ + 1) * 16], I["lam_re"][l, d_].rearrange("(gp g2) p -> (g2 p) gp", g2=2),
              writes=P_, allow_slow_non_contiguous=True)
        k.dma("sp", liA[:, d_ * 16:(d_ + 1) * 16], I["lam_im"][l, d_].rearrange("(gp g2) p -> (g2 p) gp", g2=2),
              writes=P_, allow_slow_non_contiguous=True)
    for g2 in range(2):
        k.dma("sp", ldA[g2 * 64:(g2 + 1) * 64, :].rearrange("p (d g) -> p d g", d=2),
              I["log_dt"][l].rearrange("d (gp g2) -> g2 d gp", g2=2)[g2].partition_broadcast(64),
              writes=P_, allow_slow_non_contiguous=True)
    dA = disc(lrA, liA, ldA, [32])
    magA = k.sb([128, 32], F32)
    _copy(k, "dve", magA[:], dA["mag"][:], P_, P_)
    pc = k.sb([128, 32], F32); psn = k.sb([128, 32], F32); q1 = k.sb([128, 32], F32); q2 = k.sb([128, 32], F32)
    tA = k.sb([128, 32, 64], F32); tB = k.sb([128, 32, 64], F32)
    _copy(k, "dve", pc[:], dA["c"][:], P_, P_)
    _copy(k, "dve", psn[:], dA["s"][:], P_, P_)
    v3 = lambda t_: t_.rearrange("p (d g) -> p d g", d=2)
    _copy(k, "dve", Ctab[:, :, :, 0], v3(pc[:]), P_, P_ + [TT("s5tab")])
    _copy(k, "dve", Stab[:, :, :, 0], v3(psn[:]), P_, P_ + [TT("s5tab")])
    tA4 = tA[:].rearrange("p (d g) t -> p d g t", d=2)
    tB4 = tB[:].rearrange("p (d g) t -> p d g t", d=2)
    blk = 1
    while blk < 128:
        pcb = v3(pc[:]).unsqueeze(3).to_broadcast([128, 2, 16, blk])
        psb = v3(psn[:]).unsqueeze(3).to_broadcast([128, 2, 16, blk])
        C0, S0 = Ctab[:, :, :, 0:blk], Stab[:, :, :, 0:blk]
        ew(tA4[:, :, :, 0:blk], C0, pcb, ALU.mult)
        ew(tB4[:, :, :, 0:blk], S0, psb, ALU.mult)
        ew(Ctab[:, :, :, blk:2 * blk], tA4[:, :, :, 0:blk], tB4[:, :, :, 0:blk], ALU.subtract)
        ew(tA4[:, :, :, 0:blk], C0, psb, ALU.mult)
        ew(tB4[:, :, :, 0:blk], S0, pcb, ALU.mult)
        ew(Stab[:, :, :, blk:2 * blk], tA4[:, :, :, 0:blk], tB4[:, :, :, 0:blk], ALU.add)
        ew(q1[:], pc[:], pc[:], ALU.mult)
        ew(q2[:], psn[:], psn[:], ALU.mult)
        ew(psn[:], psn[:], pc[:], ALU.mult)
        ts(psn[:], psn[:], 2.0, None, ALU.mult)
        ew(pc[:], q1[:], q2[:], ALU.subtract)
        blk *= 2
    CrA = k.sb([128, 16, 16], F32); CiA = k.sb([128, 16, 16], F32)
    for g2 in range(2):
        for gp_ in range(16):
            k.dma("sp", CrA[g2 * 64:(g2 + 1) * 64, gp_, :], I["c_re"][l, 2 * gp_ + g2].rearrange("h p -> p h"), writes=P_, allow_slow_non_contiguous=True)
            k.dma("sp", CiA[g2 * 64:(g2 + 1) * 64, gp_, :], I["c_im"][l, 2 * gp_ + g2].rearrange("h p -> p h"), writes=P_, allow_slow_non_contiguous=True)
    k.op("dve", lambda e: e.memset(RL[:], 0.0), P_, P_)
    for gq in range(4):
        for g2 in range(2):
            ps_ = slice(g2 * 64, (g2 + 1) * 64)
            cs_ = slice(gq * 32 + g2 * 16, gq * 32 + g2 * 16 + 16)
            _copy(k, "dve", RL[ps_, 0, gq::4, cs_], CrA[ps_, gq::4, :], P_, P_)
            k.op("dve", lambda e, o=RL[ps_, 1, gq::4, cs_], i=CiA[ps_, gq::4, :]: e.tensor_scalar(o, i, -1.0, None, ALU.mult), P_, P_)
    lrB = k.sb([128, 2, 4, 64], F32); liB = k.sb([128, 2, 4, 64], F32); ldB = k.sb([128, 2, 4, 64], F32)
    ldb0 = k.sb([128, 2, 4], F32)
    brB = k.sb([128, 4, 64], F32); biB = k.sb([128, 4, 64], F32)
    for q8 in range(8):
        ps_ = slice(q8 * 16, (q8 + 1) * 16)
        k.dma("sp", lrB[ps_], I["lam_re"][l][:, q8::8, :].partition_broadcast(16), writes=P_)
        k.dma("sp", liB[ps_], I["lam_im"][l][:, q8::8, :].partition_broadcast(16), writes=P_)
        k.dma("sp", ldb0[ps_], I["log_dt"][l][:, q8::8].partition_broadcast(16), writes=P_, allow_slow_non_contiguous=True)
        for ft_ in range(4):
            k.dma("sp", brB[ps_, ft_, :], I["b_re"][l, ft_ * 8 + q8].rearrange("p h -> h p"), writes=P_, allow_slow_non_contiguous=True)
            k.dma("sp", biB[ps_, ft_, :], I["b_im"][l, ft_ * 8 + q8].rearrange("p h -> h p"), writes=P_, allow_slow_non_contiguous=True)
    _copy(k, "dve", ldB[:], ldb0[:].unsqueeze(3).to_broadcast([128, 2, 4, 64]), P_, P_)
    dB = disc(lrB, liB, ldB, [2, 4, 64])
    bbr = k.sb([128, 2, 4, 64], F32); bbi = k.sb([128, 2, 4, 64], F32)
    u1 = k.sb([128, 2, 4, 64], F32); u2 = k.sb([128, 2, 4, 64], F32)
    brb = brB[:].unsqueeze(1).to_broadcast([128, 2, 4, 64]); bib = biB[:].unsqueeze(1).to_broadcast([128, 2, 4, 64])
    ew(u1[:], dB["fr"][:], brb, ALU.mult); ew(u2[:], dB["fi"][:], bib, ALU.mult); ew(bbr[:], u1[:], u2[:], ALU.subtract)
    ew(u1[:], dB["fr"][:], bib, ALU.mult); ew(u2[:], dB["fi"][:], brb, ALU.mult); ew(bbi[:], u1[:], u2[:], ALU.add)
    mkb = C.mk[:, 0:8].rearrange("p (a b) -> p a b", b=2).unsqueeze(3).to_broadcast([128, 4, 2, 64])
    for d in range(2):
        for ri, bb in enumerate((bbr, bbi)):
            for ft in range(4):
                src = bb[:, d, ft, :].unsqueeze(1).unsqueeze(1).to_broadcast([128, 4, 2, 64])
                k.op("dve", lambda e, o=XL[:, d, ri, ft], s=src: e.tensor_tensor(o, s, mkb, ALU.mult), P_ + [TT("mk")], P_)
    k.barrier()
    k.sb_off = mark2
    magA2 = k.sb([128, 32], F32)
    _copy(k, "dve", magA2[:], dA["mag"][:], [], [TT("magA2")])
    k.barrier()
    _copy(k, "dve", RT[:], magA2[:].unsqueeze(2).to_broadcast([128, 32, 128]), [TT("magA2")], [TT("s5RT")])
    k.op("dve", lambda e: e.memset(RT[:, :, 0:1], 0.0), [TT("s5RT")], [TT("s5RT")])

    X = [k.sb([128, 2, 16, 128], F32) for _ in range(2)]
    TA = [k.sb([128, 2, 16, 128], F32) for _ in range(2)]
    TB = [k.sb([128, 2, 16, 128], F32) for _ in range(2)]
    Hb = [k.sb([128, 2, 16, 128], BF16) for _ in range(2)]
    cry = [k.sb([128, 2, 16], F32) for _ in range(2)]
    c1 = [k.sb([128, 16], F32) for _ in range(2)]
    c2 = [k.sb([128, 16], F32) for _ in range(2)]
    c3 = [k.sb([128, 2, 16], F32) for _ in range(2)]
    CSl = [k.sb([128, 2, 16], F32) for _ in range(2)]
    for d_ in range(2):
        _tt(k, "dve", CSl[d_][:], CS[:, d_, :, :, 127], magA2[:, d_ * 16:(d_ + 1) * 16].unsqueeze(1).to_broadcast([128, 2, 16]), ALU.mult,
            [TT("s5tab"), TT("magA2")], [TT("s5CSl")])
    ub = [k.sb([128, 4, 128], BF16) for _ in range(2)]
    ysb = [k.sb([128, 4, 128], F32) for _ in range(2)]
    chain = [list(range(NTILE)), [1, 0] + list(range(NTILE - 1, 1, -1))]
    for d in range(2):
        k.op("dve" if d == 0 else "pool", lambda e, d=d: e.memset(cry[d][:], 0.0), [], [TT("s5c", d)])
    def st_x(it, d):
        eng = "dve" if d == 0 else "pool"
        ti = chain[d][it]
        mi = mt_of_tile(ti)
        cs = slice(ti * 128, (ti + 1) * 128)
        k.dma("sp", ub[d][:], S["zT"][6 * 128:10 * 128, cs].rearrange("(ft p) t -> p ft t", p=128),
              reads=[TT("zT", 6 + ft, mi) for ft in range(4)], writes=[TT("s5u", d)])
        for j in range(8):
            pb = d * 3 + (j % 3)
            for gpl in range(2):
                gp = 2 * j + gpl
                for ri in range(2):
                    slot = ri * 2 + gpl
                    lhsT = XL[:, d, ri, gp // 4, gp % 4].rearrange("p a b -> p (a b)")
                    _mm(k, C.ps[pb][:, slot * 128:(slot + 1) * 128], lhsT, ub[d][:, gp // 4, :], True, True,
                        [TT("s5p"), TT("s5u", d)], [TT("ps", pb)])
            dst = X[d][:, :, 2 * j:2 * j + 2, :] if d == 0 else X[d][:, :, 2 * j:2 * j + 2, ::-1]
            _copy(k, "act", dst, C.ps[pb][:, 0:512].rearrange("p (r g t) -> p r g t", r=2, g=2), [TT("ps", pb)], [TT("s5X", d, j)])

    def st_rot(it, d):
        eng = "dve" if d == 0 else "pool"
        ti = chain[d][it]
        mi = mt_of_tile(ti)
        cs = slice(ti * 128, (ti + 1) * 128)
        CSd = CS[:, d]
        allX = [TT("s5X", d, j) for j in range(8)]
        allA = [TT("s5A_", d, ri, gp) for ri in range(2) for gp in range(16)]
        tB_, tH, tc = TT("s5B", d), TT("s5H", d), TT("s5c", d)
        tab = TT("s5tab")
        Xd, A_, B_ = X[d], TA[d], TB[d]
        _tt(k, eng, A_[:], Xd[:], CSd, ALU.mult, allX + [tab], allA)
        _tt(k, eng, B_[:], Xd[:, ::-1], CSd, ALU.mult, allX + [tab], [tB_])
        _tt(k, eng, Xd[:, 0], A_[:, 0], A_[:, 1], ALU.add, allA, allX)
        _tt(k, eng, Xd[:, 1], B_[:, 0], B_[:, 1], ALU.subtract, [tB_], allX)

    def st_rest(it, d):
        eng = "dve" if d == 0 else "pool"
        ti = chain[d][it]
        mi = mt_of_tile(ti)
        cs = slice(ti * 128, (ti + 1) * 128)
        CSd = CS[:, d]
        allX = [TT("s5X", d, j) for j in range(8)]
        allA = [TT("s5A_", d, ri, gp) for ri in range(2) for gp in range(16)]
        tB_, tH, tc = TT("s5B", d), TT("s5H", d), TT("s5c", d)
        tab = TT("s5tab")
        Xd, A_, B_ = X[d], TA[d], TB[d]
        _tt(k, "dve", Xd[:, :, :, 0], Xd[:, :, :, 0], cry[d][:], ALU.add, allX + [tc], allX)
        for ri in range(2):
            k.op("dve", lambda e, o=A_[:, ri].rearrange("p g t -> p (g t)"), a=RT[:, d * 16:(d + 1) * 16, :].rearrange("p g t -> p (g t)"),
                 w=Xd[:, ri].rearrange("p g t -> p (g t)"): e.tensor_tensor_scan(o, a, w, 0.0, ALU.mult, ALU.add),
                 allX + [TT("s5RT")], [TT("s5A_", d, ri, gp) for gp in range(16)])
        GL = A_[:, :, :, 127]
        GLs = A_[:, ::-1, :, 127]
        _tt(k, "dve", c3[d][:], GL, CSl[d][:], ALU.mult, allA + [TT("s5CSl")], [TT("s5c1", d)])
        _tt(k, "dve", cry[d][:, 0, :], c3[d][:, 0, :], c3[d][:, 1, :], ALU.subtract, [TT("s5c1", d)], [tc])
        _tt(k, "dve", c3[d][:], GLs, CSl[d][:], ALU.mult, allA + [TT("s5CSl"), TT("s5c1", d)], [TT("s5c1", d)])
        _tt(k, "dve", cry[d][:, 1, :], c3[d][:, 0, :], c3[d][:, 1, :], ALU.add, [TT("s5c1", d)], [tc])
        HR = Hb[d][:, 0] if d == 0 else Hb[d][:, 0, :, ::-1]
        HI = Hb[d][:, 1] if d == 0 else Hb[d][:, 1, :, ::-1]
        _tt(k, "dve", B_[:], A_[:], CSd, ALU.mult, allA + [tab], [tB_])
        _tt(k, "dve", Xd[:], A_[:, ::-1], CSd, ALU.mult, allA + [tab], allX)
        _tt(k, "dve", HR, B_[:, 0], B_[:, 1], ALU.subtract, [tB_], [tH])
        _tt(k, "dve", HI, Xd[:, 0], Xd[:, 1], ALU.add, allX, [tH])
        pb = 6 + d
        for ft in range(4):
            n_ = 0
            for gq in range(4):
                gp = ft * 4 + gq
                for ri in range(2):
                    _mm(k, C.ps[pb][:, ft * 128:(ft + 1) * 128], RL[:, ri, gp, :], Hb[d][:, ri, gp, :], n_ == 0, n_ == 7,
                        [TT("s5p"), tH], [TT("ps", pb)])
                    n_ += 1
        _copy(k, "act", ysb[d][:], C.ps[pb][:, 0:512].rearrange("p (f t) -> p f t", f=4), [TT("ps", pb)], [TT("s5y", d)])
        dstD = S["yf"] if d == 0 else S["yb"]
        k.dma("act", dstD[:, cs].rearrange("(ft p) t -> p ft t", p=128), ysb[d][:], reads=[TT("s5y", d)],
              writes=[TT("yfb", d, ti)])


    st_x(0, 0)
    st_x(0, 1)
    st_rot(0, 1)
    for it in range(NTILE):
        st_rot(it, 0)
        st_rest(it, 0)
        if it + 1 < NTILE:
            st_x(it + 1, 0)
        st_rest(it, 1)
        if it + 1 < NTILE:
            st_x(it + 1, 1)
            st_rot(it + 1, 1)
    k.barrier()
    k.reset()
    wglu = k.sb([128, 4, 512], BF16)
    load_w_bf(C, wglu[:], I["w_glu"][l].rearrange("(kt p) c -> p kt c", p=128), TT("wglu"))
    bgl = k.sb([128, 4], F32); dsk2 = k.sb([128, 4], F32)
    k.dma("sp", bgl[:], I["b_glu"][l].rearrange("(ft p) -> p ft", p=128), writes=[TT("bgl")], allow_slow_non_contiguous=True)
    k.dma("sp", dsk2[:], I["ssm_d"][l].rearrange("(ft p) -> p ft", p=128), writes=[TT("dsk2")], allow_slow_non_contiguous=True)
    yf = k.sb([128, 4, 512], F32); yb = k.sb([128, 4, 512], F32); uu = k.sb([128, 4, 512], BF16)
    x2 = k.sb([128, 4, 512], F32); sg = k.sb([128, 4, 512], F32); g = k.sb([128, 4, 512], BF16)
    s2 = [k.sb([128, 512], F32) for _ in range(2)]
    Ao = k.sb([128, 4, 512], BF16)
    c0_ = 2.0 * math.sqrt(2.0 / math.pi)
    c1_ = c0_ * 0.044715
    for mi, (c0, n, isctx) in enumerate(C.mts):
        tiles = list(range(c0 // 128, (c0 + n) // 128))
        cs = slice(c0, c0 + n)
        k.dma("sp", yf[:, :, 0:n], S["yf"][:, cs].rearrange("(ft p) t -> p ft t", p=128), reads=[TT("yfb", 0, ti) for ti in tiles], writes=[TT("o_yf")])
        k.dma("sp", yb[:, :, 0:n], S["yb"][:, cs].rearrange("(ft p) t -> p ft t", p=128), reads=[TT("yfb", 1, ti) for ti in tiles], writes=[TT("o_yb")])
        k.dma("sp", uu[:, :, 0:n], S["zT"][6 * 128:10 * 128, cs].rearrange("(ft p) t -> p ft t", p=128),
              reads=[TT("zT", 6 + ft, mi) for ft in range(4)], writes=[TT("o_u")])
        _tt(k, "dve", yf[:, :, 0:n], yf[:, :, 0:n], yb[:, :, 0:n], ALU.add, [TT("o_yf"), TT("o_yb")], [TT("o_yf")])
        for ft in range(4):
            k.op("dve", lambda e, ft=ft, n=n: e.scalar_tensor_tensor(yf[:, ft, 0:n], uu[:, ft, 0:n], dsk2[:, ft:ft + 1], yf[:, ft, 0:n], ALU.mult, ALU.add),
                 [TT("o_u"), TT("o_yf"), TT("dsk2")], [TT("o_yf")])
        _tt(k, "pool", x2[:, :, 0:n], yf[:, :, 0:n], yf[:, :, 0:n], ALU.mult, [TT("o_yf")], [TT("o_x2")])
        k.op("pool", lambda e, n=n: e.tensor_scalar(x2[:, :, 0:n], x2[:, :, 0:n], c1_, c0_, ALU.mult, ALU.add), [TT("o_x2")], [TT("o_x2")])
        _tt(k, "pool", x2[:, :, 0:n], x2[:, :, 0:n], yf[:, :, 0:n], ALU.mult, [TT("o_x2"), TT("o_yf")], [TT("o_x2")])
        _act(k, sg[:, :, 0:n], x2[:, :, 0:n], AF.Sigmoid, [TT("o_x2")], [TT("o_sg")])
        _tt(k, "dve", g[:, :, 0:n], yf[:, :, 0:n], sg[:, :, 0:n], ALU.mult, [TT("o_yf"), TT("o_sg")], [TT("o_g")])
        for ot in range(4):
            pb = ot
            for kt in range(4):
                _mm(k, C.ps[pb][:, 0:n], wglu[:, kt, ot * 128:(ot + 1) * 128], g[:, kt, 0:n], kt == 0, kt == 3,
                    [TT("wglu"), TT("o_g")], [TT("ps", pb)])
            _act(k, s2[ot % 2][:, 0:n], C.ps[pb][:, 0:n], AF.Sigmoid, [TT("ps", pb), TT("bgl")], [TT("o_s2", ot % 2)], bias=bgl[:, ot:ot + 1])
            _tt(k, "dve", Ao[:, ot, 0:n], g[:, ot, 0:n], s2[ot % 2][:, 0:n], ALU.mult, [TT("o_g"), TT("o_s2", ot % 2)], [TT("o_A")])
        k.dma("pool", S["s5A"][:, cs].rearrange("(ft p) t -> p ft t", p=128), Ao[:, :, 0:n], reads=[TT("o_A")], writes=[TT("s5A", mi)])
    k.barrier()
    k.reset()


import os as _os
_GLA_CUT = int(_os.environ.get("GLA_CUT", "0"))


def _gla_end(C):
    C.k.barrier()
    C.k.reset()


def phase_gla(C, l):
    k, nc, I, S = C.k, C.nc, C.I, C.S
    TT = k.t
    NTILE = C.NTILE
    IDN, TRIU, TRIL, TRISU, TRISL = (slice(i * 128, (i + 1) * 128) for i in range(5))
    wa = k.sb([128, 256], BF16)
    ba = k.sb([128, 2, 256], BF16)
    k.dma("pool", wa[0:16, :], I["w_a2"][l, 0], writes=[TT("wa")])
    k.dma("pool", wa[32:48, :], I["w_a2"][l, 1], writes=[TT("wa")])
    k.dma("pool", ba[0:1, :, :], I["b_a2"][l].partition_broadcast(1), writes=[TT("ba")])
    k.dma("pool", ba[32:33, :, :], I["b_a2"][l].partition_broadcast(1), writes=[TT("ba")])
    gnm = k.sb([128, 4], F32)
    k.dma("sp", gnm[:], I["g_norm"][l].rearrange("(h p) -> p h", p=128), writes=[TT("gnm")], allow_slow_non_contiguous=True)
    Sf = [k.sb([128, 2, 128], F32) for _ in range(2)]
    Sb = [k.sb([128, 2, 128], BF16) for _ in range(2)]
    aT = k.sb([128, 128], BF16); gq = k.sb([128, 2, 128], BF16); gk = k.sb([128, 2, 128], BF16)
    gkv = k.sb([128, 768], BF16); gr = k.sb([128, 4, 128], BF16); ofl = k.sb([128, 4, 128], F32)
    e1 = k.sb([128, 256], F32); l1 = k.sb([128, 256], F32)
    eq = k.sb([128, 2, 128], F32); ek = k.sb([128, 2, 128], F32)
    qin = k.sb([128, 2, 128], BF16); kin = k.sb([128, 2, 128], BF16)
    attm = k.sb([128, 4, 128], BF16)
    ed = k.sb([128, 256], F32); kdec = k.sb([128, 256], BF16)
    osb = k.sb([128, 4, 128], F32); osq = k.sb([128, 4, 128], BF16); rs = k.sb([128, 512], F32)
    on = k.sb([128, 4, 128], F32); oa = k.sb([128, 4, 128], BF16)
    chain = [list(range(NTILE)), [1, 0] + list(range(NTILE - 1, 1, -1))]
    for d in range(2):
        tri_f = C.cst[:, TRIU] if d == 0 else C.cst[:, TRIL]
        tris_f = C.cst[:, TRISU] if d == 0 else C.cst[:, TRISL]
        mask_f = C.cst[:, TRIU] if d == 0 else C.cst[:, TRIL]
        last = 127 if d == 0 else 0
        ar = slice(d * 32, d * 32 + 16)
        k.op("dve", lambda e, d=d: e.memset(Sf[d][:], 0.0), [], [TT("gSf", d)])
        k.op("dve", lambda e, d=d: e.memset(Sb[d][:], 0.0), [], [TT("gSb", d)])
        for it in range(NTILE):
            ti = chain[d][it]
            mi = mt_of_tile(ti)
            cs = slice(ti * 128, (ti + 1) * 128)
            zT = S["zT"]
            k.dma("sp", aT[0:64, :], zT[4 * 128:4 * 128 + 64, cs], reads=[TT("zT", 4, mi)], writes=[TT("g_aT")])
            k.dma("sp", gq[:], zT[10 * 128:12 * 128, cs].rearrange("(a p) t -> p a t", p=128),
                  reads=[TT("zT", 10, mi), TT("zT", 11, mi)], writes=[TT("g_gq")])
            k.dma("sp", gk[:], zT[12 * 128:14 * 128, cs].rearrange("(a p) t -> p a t", p=128),
                  reads=[TT("zT", 12, mi), TT("zT", 13, mi)], writes=[TT("g_gk")])
            k.dma("sp", gkv[:], S["gkv"][cs, :], reads=[TT("gkv", ti)], writes=[TT("g_gkv")])
            if d == 1:
                k.dma("sp", gr[:], zT[14 * 128:18 * 128, cs].rearrange("(a p) t -> p a t", p=128),
                      reads=[TT("zT", 14 + a, mi) for a in range(4)], writes=[TT("g_gr")])
                k.dma("sp", ofl[:], S["of"][:, cs].rearrange("(a p) t -> p a t", p=128), reads=[TT("of", ti)], writes=[TT("g_of")])
            _mm(k, C.ps[0][:, 0:256], aT[ar, :], wa[ar, :], True, False, [TT("g_aT"), TT("wa")], [TT("ps", 0)])
            _mm(k, C.ps[0][:, 0:256], C.onesb[d * 32:d * 32 + 1, :], ba[d * 32:d * 32 + 1, d, :], False, True, [TT("onesb"), TT("ba")], [TT("ps", 0)])
            if _GLA_CUT == 1:
                return _gla_end(C)
            _act(k, e1[:], C.ps[0][:, 0:256], AF.Exp, [TT("ps", 0)], [TT("g_e1")], scale=-1.0)
            _act(k, l1[:], e1[:], AF.Ln, [TT("g_e1"), TT("one")], [TT("g_l1")], bias=C.oneT[:, 0:1])
            if _GLA_CUT == 2:
                return _gla_end(C)
            for pt in range(2):
                _mm(k, C.ps[1][:, pt * 128:(pt + 1) * 128], l1[:, pt * 128:(pt + 1) * 128], tri_f, True, True,
                    [TT("g_l1"), TT("cst")], [TT("ps", 1)])
            cps = C.ps[1][:, 0:256].rearrange("p (a t) -> p a t", a=2)
            _act(k, eq[:], cps, AF.Exp, [TT("ps", 1)], [TT("g_eq")], scale=-1.0 / 16)
            _act(k, ek[:], cps, AF.Exp, [TT("ps", 1)], [TT("g_ek")], scale=1.0 / 16)
            k.op("dve", lambda e: e.scalar_tensor_tensor(qin[:], gq[:], 0.125, eq[:], ALU.mult, ALU.mult),
                 [TT("g_gq"), TT("g_eq")], [TT("g_qin")])
            _tt(k, "pool", kin[:], gk[:], ek[:], ALU.mult, [TT("g_gk"), TT("g_ek")], [TT("g_kin")])
            if _GLA_CUT == 3:
                return _gla_end(C)
            for h in range(4):
                hr = slice((h % 2) * 64, (h % 2) * 64 + 64)
                pbk = 2 if h % 2 == 0 else 7
                _mm(k, C.ps[pbk][:, (h // 2) * 128:(h // 2 + 1) * 128], kin[hr, h // 2, :], qin[hr, h // 2, :], True, True,
                    [TT("g_kin"), TT("g_qin")], [TT("ps", pbk)])
            for par, pbk in ((0, 2), (1, 7)):
                _tt(k, "dve", attm[:, par::2, :], C.ps[pbk][:, 0:256].rearrange("p (h t) -> p h t", h=2),
                    mask_f.unsqueeze(1).to_broadcast([128, 2, 128]), ALU.mult, [TT("ps", pbk), TT("cst")], [TT("g_attm")])
            if _GLA_CUT == 5:
                return _gla_end(C)
            _mm(k, C.ps[3][:, 0:256], tris_f, l1[:], True, True, [TT("g_l1"), TT("cst")], [TT("ps", 3)])
            _act(k, ed[:], C.ps[3][:, 0:256], AF.Exp, [TT("ps", 3)], [TT("g_ed")], scale=-1.0 / 16)
            _tt(k, "pool", kdec[:], gkv[:, 0:256], ed[:], ALU.mult, [TT("g_gkv"), TT("g_ed")], [TT("g_kdec")])
            if _GLA_CUT == 6:
                return _gla_end(C)
            for pt in range(2):
                _mm(k, C.ps[4][:, pt * 256:(pt + 1) * 256], kdec[:, pt * 128:(pt + 1) * 128], gkv[:, 256 + pt * 256:256 + (pt + 1) * 256],
                    True, True, [TT("g_kdec"), TT("g_gkv")], [TT("ps", 4)])
            if _GLA_CUT == 7:
                return _gla_end(C)
            for h in range(4):
                hr = slice((h % 2) * 64, (h % 2) * 64 + 64)
                _mm(k, C.ps[5][:, h * 128:(h + 1) * 128], gkv[:, 256 + h * 128:256 + (h + 1) * 128], attm[:, h, :], True, False,
                    [TT("g_gkv"), TT("g_attm")], [TT("ps", 5)])
                _mm(k, C.ps[5][:, h * 128:(h + 1) * 128], Sb[d][hr, h // 2, :], qin[hr, h // 2, :], False, True,
                    [TT("gSb", d), TT("g_qin")], [TT("ps", 5)])
            if _GLA_CUT == 9:
                return _gla_end(C)
            for h in range(4):
                hr = slice((h % 2) * 64, (h % 2) * 64 + 64)
                pt = h // 2
                kvb = C.ps[4][hr, pt * 256 + (h % 2) * 128: pt * 256 + (h % 2) * 128 + 128]
                k.op("dve", lambda e, hr=hr, pt=pt, kvb=kvb, d=d, last=last: e.scalar_tensor_tensor(Sf[d][hr, pt, :], Sf[d][hr, pt, :], eq[hr, pt, last:last + 1], kvb, ALU.mult, ALU.add),
                     [TT("gSf", d), TT("g_eq"), TT("ps", 4)], [TT("gSf", d)])
            _copy(k, "pool", Sb[d][:], Sf[d][:], [TT("gSf", d)], [TT("gSb", d)])
            if _GLA_CUT == 8:
                return _gla_end(C)
            ops_ = C.ps[5][:, 0:512].rearrange("p (h t) -> p h t", h=4)
            if d == 0:
                _copy(k, "act", osb[:], ops_, [TT("ps", 5)], [TT("g_osb")])
                k.dma("pool", S["of"][:, cs].rearrange("(a p) t -> p a t", p=128), osb[:], reads=[TT("g_osb")], writes=[TT("of", ti)])
            else:
                _tt(k, "dve", osb[:], ops_, ofl[:], ALU.add, [TT("ps", 5), TT("g_of")], [TT("g_osb")])
                _act(k, osq[:], osb[:], AF.Square, [TT("g_osb")], [TT("g_osq")])
                _mm(k, C.ps[6][:, 0:512], C.onesb[:], osq[:].rearrange("p h t -> p (h t)"), True, True, [TT("g_osq"), TT("onesb")], [TT("ps", 6)])
                _act(k, rs[:], C.ps[6][:, 0:512], AF.Sqrt, [TT("ps", 6), TT("eps")], [TT("g_rs")], bias=C.epsT[:, 0:1], scale=1.0 / 128)
                k.op("dve", lambda e: e.reciprocal(rs[:], rs[:]), [TT("g_rs")], [TT("g_rs")])
                _tt(k, "dve", on[:], osb[:], rs[:].rearrange("p (h t) -> p h t", h=4), ALU.mult, [TT("g_osb"), TT("g_rs")], [TT("g_on")])
                for h in range(4):
                    k.op("dve", lambda e, h=h: e.scalar_tensor_tensor(oa[:, h, :], on[:, h, :], gnm[:, h:h + 1], gr[:, h, :], ALU.mult, ALU.mult),
                         [TT("g_on"), TT("gnm"), TT("g_gr")], [TT("g_oa")])
                k.dma("pool", S["glaA"][:, cs].rearrange("(a p) t -> p a t", p=128), oa[:], reads=[TT("g_oa")], writes=[TT("glaA", ti)])
    k.barrier()
    k.reset()


def phase_E1(C, l, last):
    k, nc, I, S = C.k, C.nc, C.I, C.S
    TT = k.t
    wo = [k.sb([128, 4, D], BF16) for _ in range(3)]
    for i, nm in enumerate(("w_o", "s_wo", "g_wo")):
        load_w_bf(C, wo[i][:], I[nm][l].rearrange("(kt p) c -> p kt c", p=128), TT("wo", i))
    wout = k.sb([128, 8, D], BF16)
    load_w_bf(C, wout[:], I["w_out"][l].rearrange("(kt p) c -> p kt c", p=128), TT("wout"), 2)
    br = [k.sb([128, 4, 512], BF16) for _ in range(3)]
    gt = k.sb([128, 24, 512], BF16)
    xt = k.sb([128, 8, 512], F32)
    mT = k.sb([128, 8, 512], BF16)
    ta = k.sb([128, 512], F32); tb = k.sb([128, 512], F32); tc_ = k.sb([128, 512], F32)
    xTv = S["xT"].rearrange("(kt p) t -> p kt t", p=128)
    for mi, (c0, n, isctx) in enumerate(C.mts):
        if last and isctx:
            continue
        cs = slice(c0, c0 + n)
        col = 1 if isctx else 0
        tiles = list(range(c0 // 128, (c0 + n) // 128))
        k.dma("sp", br[0][:, :, 0:n], S["attO"][:, cs].rearrange("(a p) t -> p a t", p=128), reads=[TT("attO", mi)], writes=[TT("e_br", 0)])
        k.dma("sp", br[1][:, :, 0:n], S["s5A"][:, cs].rearrange("(a p) t -> p a t", p=128), reads=[TT("s5A", mi)], writes=[TT("e_br", 1)])
        k.dma("sp", br[2][:, :, 0:n], S["glaA"][:, cs].rearrange("(a p) t -> p a t", p=128), reads=[TT("glaA", ti) for ti in tiles], writes=[TT("e_br", 2)])
        k.dma("sp", gt[:, :, 0:n], S["zT"][18 * 128:42 * 128, cs].rearrange("(a p) t -> p a t", p=128),
              reads=[TT("zT", 18 + a, mi) for a in range(24)], writes=[TT("e_gt")])
        k.dma("sp", xt[:, :, 0:n], xTv[:, :, cs], reads=[TT("xT", mi)], writes=[TT("e_xt")])
        for ot in range(8):
            for i in range(3):
                for kt in range(4):
                    _mm(k, C.ps[i][:, 0:n], wo[i][:, kt, ot * 128:(ot + 1) * 128], br[i][:, kt, 0:n], kt == 0, kt == 3,
                        [TT("wo", i), TT("e_br", i)], [TT("ps", i)])
            _tt(k, "dve", ta[:, 0:n], C.ps[0][:, 0:n], gt[:, ot, 0:n], ALU.mult, [TT("ps", 0), TT("e_gt")], [TT("e_ta")])
            _tt(k, "dve", tb[:, 0:n], C.ps[1][:, 0:n], gt[:, 8 + ot, 0:n], ALU.mult, [TT("ps", 1), TT("e_gt")], [TT("e_tb")])
            _tt(k, "dve", tc_[:, 0:n], C.ps[2][:, 0:n], gt[:, 16 + ot, 0:n], ALU.mult, [TT("ps", 2), TT("e_gt")], [TT("e_tc")])
            _tt(k, "pool", ta[:, 0:n], ta[:, 0:n], tb[:, 0:n], ALU.add, [TT("e_ta"), TT("e_tb")], [TT("e_ta")])
            _tt(k, "pool", mT[:, ot, 0:n], ta[:, 0:n], tc_[:, 0:n], ALU.add, [TT("e_ta"), TT("e_tc")], [TT("e_mT")])
        for ot in range(8):
            pb = 3 + ot % 4
            for kt in range(8):
                _mm(k, C.ps[pb][:, 0:n], wout[:, kt, ot * 128:(ot + 1) * 128], mT[:, kt, 0:n], kt == 0, kt == 7,
                    [TT("wout"), TT("e_mT")], [TT("ps", pb)])
            k.op("dve", lambda e, ot=ot, pb=pb, n=n, col=col: e.scalar_tensor_tensor(xt[:, ot, 0:n], C.ps[pb][:, 0:n], C.vec[:, 2, ot, col:col + 1], xt[:, ot, 0:n], ALU.mult, ALU.add),
                 [TT("ps", pb), TT("vec"), TT("e_xt")], [TT("e_xt")])
        k.dma("pool", xTv[:, :, cs], xt[:, :, 0:n], reads=[TT("e_xt")], writes=[TT("xT", mi)])
    k.barrier()
    k.reset()


def phase_E2(C, l, last):
    k, nc, I, S = C.k, C.nc, C.I, C.S
    TT = k.t
    wup = k.sb([128, 8, 5632], BF16)
    wdn = k.sb([128, 22, D], BF16)
    load_w_bf(C, wup[:], I["w_up"][l].rearrange("(kt p) c -> p kt c", p=128), TT("wup"), 8)
    load_w_bf(C, wdn[:], I["w_dn"][l].rearrange("(kt p) c -> p kt c", p=128), TT("wdn"), 2)
    xt = k.sb([128, 8, 512], F32)
    hT = k.sb([128, 8, 512], BF16)
    a = k.sb([128, 22, 512], BF16)
    rs = k.sb([128, 512], F32)
    tmp = [k.sb([128, 512], F32) for _ in range(2)]
    sgt = [k.sb([128, 512], F32) for _ in range(2)]
    fn = k.sb([128, 8], F32)
    k.dma("sp", fn[:], I["fnorm"].rearrange("(kt p) -> p kt", p=128), writes=[TT("fn")], allow_slow_non_contiguous=True)
    ident = C.cst[:, 0:128]
    xTv = S["xT"].rearrange("(kt p) t -> p kt t", p=128)
    for mi, (c0, n, isctx) in enumerate(C.mts):
        if last and isctx:
            continue
        cs = slice(c0, c0 + n)
        col = 1 if isctx else 0
        k.dma("sp", xt[:, :, 0:n], xTv[:, :, cs], reads=[TT("xT", mi)], writes=[TT("f_xt")])
        sq = a[:, 0:8, :]
        _act(k, sq[:, :, 0:n], xt[:, :, 0:n], AF.Square, [TT("f_xt")], [TT("f_a")])
        colsum_rstd(C, [sq[:, kt, 0:n] for kt in range(8)], n, rs, float(D), [TT("f_a")], TT("f_rs"))
        for kt in range(8):
            t = tmp[kt % 2]
            _tt(k, "dve", t[:, 0:n], xt[:, kt, 0:n], rs[:, 0:n], ALU.mult, [TT("f_xt"), TT("f_rs")], [TT("f_tmp", kt % 2)])
            _act(k, hT[:, kt, 0:n], t[:, 0:n], AF.Identity, [TT("f_tmp", kt % 2), TT("vec")], [TT("f_hT")],
                 bias=C.vec[:, 4, kt, col:col + 1], scale=C.vec[:, 3, kt, col:col + 1])
        for j in range(22):
            pg, pu = (2 * j) % 6, (2 * j + 1) % 6
            for kt in range(8):
                _mm(k, C.ps[pg][:, 0:n], wup[:, kt, j * 128:(j + 1) * 128], hT[:, kt, 0:n], kt == 0, kt == 7, [TT("wup"), TT("f_hT")], [TT("ps", pg)])
            for kt in range(8):
                _mm(k, C.ps[pu][:, 0:n], wup[:, kt, 2816 + j * 128:2816 + (j + 1) * 128], hT[:, kt, 0:n], kt == 0, kt == 7, [TT("wup"), TT("f_hT")], [TT("ps", pu)])
            _act(k, sgt[j % 2][:, 0:n], C.ps[pg][:, 0:n], AF.Silu, [TT("ps", pg)], [TT("f_sg", j % 2)])
            _tt(k, "dve", a[:, j, 0:n], sgt[j % 2][:, 0:n], C.ps[pu][:, 0:n], ALU.mult, [TT("f_sg", j % 2), TT("ps", pu)], [TT("f_a")])
        for ot in range(8):
            pb = 6 + ot % 2
            for j in range(22):
                _mm(k, C.ps[pb][:, 0:n], wdn[:, j, ot * 128:(ot + 1) * 128], a[:, j, 0:n], j == 0, j == 21, [TT("wdn"), TT("f_a")], [TT("ps", pb)])
            k.op("dve", lambda e, ot=ot, pb=pb, n=n, col=col: e.scalar_tensor_tensor(xt[:, ot, 0:n], C.ps[pb][:, 0:n], C.vec[:, 5, ot, col:col + 1], xt[:, ot, 0:n], ALU.mult, ALU.add),
                 [TT("ps", pb), TT("vec"), TT("f_xt")], [TT("f_xt")])
        if not last:
            k.dma("pool", xTv[:, :, cs], xt[:, :, 0:n], reads=[TT("f_xt")], writes=[TT("xT", mi)])
        else:
            sq = a[:, 0:8, :]
            _act(k, sq[:, :, 0:n], xt[:, :, 0:n], AF.Square, [TT("f_xt")], [TT("f_a")])
            colsum_rstd(C, [sq[:, kt, 0:n] for kt in range(8)], n, rs, float(D), [TT("f_a")], TT("f_rs"))
            for kt in range(8):
                k.op("dve", lambda e, kt=kt, n=n: e.scalar_tensor_tensor(xt[:, kt, 0:n], xt[:, kt, 0:n], fn[:, kt:kt + 1], rs[:, 0:n], ALU.mult, ALU.mult),
                     [TT("f_xt"), TT("fn"), TT("f_rs")], [TT("f_xt")])
            for j in range(n // 128):
                ob = j % 2
                for kt in range(8):
                    pb = kt % 6
                    k.op("pe", lambda e, o=C.ps[pb][:, 0:128], i=xt[:, kt, j * 128:(j + 1) * 128]: e.transpose(o, i, ident),
                         [TT("f_xt"), TT("cst")], [TT("ps", pb)])
                    dst = (sgt if kt < 4 else tmp)[ob][:, (kt % 4) * 128:(kt % 4 + 1) * 128]
                    tag = TT("f_sg", ob) if kt < 4 else TT("f_tmp", ob)
                    _copy(k, "act" if kt % 2 else "dve", dst, C.ps[pb][:, 0:128], [TT("ps", pb)], [tag])
                r0 = c0 - CTX + j * 128
                k.dma("pool", C.out[r0:r0 + 128, 0:512], sgt[ob][:], reads=[TT("f_sg", ob)], writes=[TT("out", r0, 0)])
                k.dma("pool", C.out[r0:r0 + 128, 512:1024], tmp[ob][:], reads=[TT("f_tmp", ob)], writes=[TT("out", r0, 1)])
    k.barrier()
    k.reset()


_NC_CACHE = {}


def _consts(T):
    cst = np.zeros((128, 1024), np.float32)
    m = np.arange(128)[:, None]
    l_ = np.arange(128)[None, :]
    cst[:, 0:128] = (m == l_)
    cst[:, 128:256] = (m <= l_)
    cst[:, 256:384] = (m >= l_)
    cst[:, 384:512] = (m > l_)
    cst[:, 512:640] = (m < l_)
    mk = np.zeros((128, 8), np.float32)
    for p in range(128):
        mk[p, (p // 32) * 2 + (p // 16) % 2] = 1.0
    rows_n = T // 64
    rows = np.repeat(np.arange(rows_n, dtype=np.float32), 64)
    cols = np.tile(np.arange(64, dtype=np.float32), rows_n)
    inv = (np.float32(10000.0) ** (-np.arange(8, dtype=np.float32) / np.float32(8))).astype(np.float32)
    ang = np.concatenate([rows[:, None] * inv, cols[:, None] * inv], axis=-1).astype(np.float32)
    cos, sin = np.cos(ang).astype(np.float32), np.sin(ang).astype(np.float32)
    cosR = np.repeat(cos.T, 2, axis=0)
    sinS = np.repeat(sin.T, 2, axis=0).copy()
    sinS[0::2] *= -1.0
    return cst, mk, np.ascontiguousarray(cosR), np.ascontiguousarray(sinS)


def _shared_inputs(inp, T):
    f = lambda a: np.ascontiguousarray(np.asarray(a, dtype=np.float32))
    w_in = f(inp["w_in"])
    L = w_in.shape[0]
    o = dict(cq=0, ckv=256, kr=512, u=544, gq=1056, gk=1312, gv=1568, gr=2080, af=2592, ab=2608, gates=2624)
    w_fm = np.zeros((L, 1024, NFM * 128), np.float32)
    w_fm[:, :, 0:256] = w_in[:, :, o["cq"]:o["cq"] + 256]
    w_fm[:, :, 256:512] = w_in[:, :, o["ckv"]:o["ckv"] + 256]
    w_fm[:, :, 512:528] = w_in[:, :, o["af"]:o["af"] + 16]
    w_fm[:, :, 544:560] = w_in[:, :, o["ab"]:o["ab"] + 16]
    w_fm[:, :, 576:608] = w_in[:, :, o["kr"]:o["kr"] + 32]
    swap = np.arange(32).reshape(16, 2)[:, ::-1].reshape(32)
    w_fm[:, :, 640 + 64:640 + 96] = w_in[:, :, o["kr"] + swap]
    w_fm[:, :, 768:1280] = w_in[:, :, o["u"]:o["u"] + 512]
    w_fm[:, :, 1280:1536] = w_in[:, :, o["gq"]:o["gq"] + 256]
    w_fm[:, :, 1536:1792] = w_in[:, :, o["gk"]:o["gk"] + 256]
    w_fm[:, :, 1792:2304] = w_in[:, :, o["gr"]:o["gr"] + 512]
    w_fm[:, :, 2304:5376] = w_in[:, :, o["gates"]:o["gates"] + 3072]
    w_tm = np.concatenate([w_in[:, :, o["gk"]:o["gk"] + 256], w_in[:, :, o["gv"]:o["gv"] + 512]], axis=2)
    w_uq = f(inp["mla_w_uq"])
    w_uqs = np.zeros_like(w_uq)
    for h in range(8):
        w_uqs[:, :, h * 96 + 64:h * 96 + 96] = w_uq[:, :, h * 96 + 64 + swap]
    cst, mk, cosR, sinS = _consts(T)
    sh = {
        "w_mod": f(inp["w_mod"]), "b_mod": f(inp["b_mod"]), "norm1": f(inp["norm1"]), "norm2": f(inp["norm2"]),
        "w_in": w_fm, "w_tm": np.ascontiguousarray(w_tm), "qn": f(inp["mla_q_norm"]), "kvn": f(inp["mla_kv_norm"]),
        "w_uq": w_uq, "w_uqs": w_uqs, "w_ukv": f(inp["mla_w_ukv"]), "w_o": f(inp["mla_w_o"]),
        "lam_re": f(inp["ssm_lam_re"]), "lam_im": f(inp["ssm_lam_im"]), "log_dt": f(inp["ssm_log_dt"]),
        "b_re": f(inp["ssm_b_re"]), "b_im": f(inp["ssm_b_im"]), "c_re": f(inp["ssm_c_re"]), "c_im": f(inp["ssm_c_im"]),
        "ssm_d": f(inp["ssm_d"]), "w_glu": f(inp["ssm_w_glu"]), "b_glu": f(inp["ssm_b_glu"]), "s_wo": f(inp["ssm_w_o"]),
        "w_a2": f(inp["gla_w_a2"]), "b_a2": f(inp["gla_b_a2"]), "g_norm": f(inp["gla_norm"]), "g_wo": f(inp["gla_w_o"]),
        "w_out": f(inp["w_out"]), "w_up": f(inp["ffn_w_up"]), "w_dn": f(inp["ffn_w_down"]), "fnorm": f(inp["final_norm"]),
        "cosR": cosR, "sinS": sinS, "cst": cst, "mk": mk,
    }
    return sh, L


def kernel(**inputs):
    x = np.asarray(inputs["x"], dtype=np.float32)
    B, T, _ = x.shape
    ctx = np.asarray(inputs["ctx"], dtype=np.float32)
    c = np.asarray(inputs["c"], dtype=np.float32)
    c_ctx = np.asarray(inputs["c_ctx"], dtype=np.float32)
    sh, L = _shared_inputs(inputs, T)
    key = (T, L)
    if key not in _NC_CACHE:
        _NC_CACHE[key] = build_program(T, L)
    nc = _NC_CACHE[key]
    in_maps = []
    for b in range(B):
        m = dict(sh)
        m["x"] = np.ascontiguousarray(x[b])
        m["ctx"] = np.ascontiguousarray(ctx[b])
        m["cc"] = np.ascontiguousarray(np.stack([c[b], c_ctx], axis=0))
        in_maps.append(m)
    res = run_bass_kernel_spmd(nc, in_maps, core_ids=list(range(B)))
    return np.stack([np.asarray(r["out"], dtype=np.float32) for r in res.results], axis=0)
```

```python
from concourse.bass_utils import run_bass_kernel_spmd
import numpy as np
import concourse.bass as bass
import concourse.mybir as mybir

F32 = mybir.dt.float32
BF16 = mybir.dt.bfloat16
ALU = mybir.AluOpType
AF = mybir.ActivationFunctionType
AX = mybir.AxisListType

NDSEM = 8


class Tile:
    __slots__ = ("name", "w", "r")

    def __init__(self, name):
        self.name = name
        self.w = None
        self.r = []


class Op:
    __slots__ = ("st", "dma", "fn", "idx", "didx", "waits", "dwaits", "sig", "cnt")


class K:
    STREAMS = ("pe", "act", "dve", "pool", "sp")

    def __init__(self, nc):
        self.nc = nc
        self.ops = {s: [] for s in self.STREAMS}
        self.ndma = {s: 0 for s in self.STREAMS}
        self.dmaops = {s: [] for s in self.STREAMS}
        self.tiles = {}
        self.sb_off = 16640
        self.sb_mark = 16640
        self.nalloc = 0
        self.psum = []
        self.newbufs = []
        self._inzero = False

    def t(self, *key):
        tl = self.tiles.get(key)
        if tl is None:
            tl = self.tiles[key] = Tile(key)
        return tl

    def sb(self, shape, dtype, name=None):
        self.nalloc += 1
        name = (name or "t") + "_%d" % self.nalloc
        esz = 4 if dtype == F32 else 2
        n = int(np.prod(shape[1:])) * esz
        n = (n + 63) // 64 * 64
        off = self.sb_off
        self.sb_off += n
        assert self.sb_off <= 229000, ("sbuf overflow", self.sb_off)
        h = self.nc.alloc_sbuf_tensor_at(name, list(shape), dtype, offset=off)
        self.newbufs.append(h)
        return h

    def zero_new(self):
        engs = ("pool", "dve")
        for i, h in enumerate(self.newbufs):
            self.op(engs[i % 2], lambda e, h=h: e.memset(h[:], 0.0), (), ())
        self.newbufs = []
        self.barrier()

    def mark(self):
        self.sb_mark = self.sb_off

    def reset(self):
        self.sb_off = self.sb_mark

    def _rec(self, st, dma, fn, reads, writes):
        if self.newbufs and not self._inzero:
            self._inzero = True
            self.zero_new()
            self._inzero = False
        op = Op()
        op.st, op.dma, op.fn = st, dma, fn
        op.idx = len(self.ops[st])
        op.sig = False
        op.cnt = 0
        op.didx = -1
        deps = set()
        for tl in reads:
            if tl.w is not None:
                deps.add(tl.w)
        for tl in writes:
            if tl.w is not None:
                deps.add(tl.w)
            deps.update(tl.r)
        for tl in reads:
            tl.r.append(op)
        for tl in writes:
            tl.w = op
            tl.r = []
        if dma:
            op.didx = self.ndma[st]
            self.ndma[st] += 1
            self.dmaops[st].append(op)
            if op.didx >= NDSEM:
                deps.add(self.dmaops[st][op.didx - NDSEM])
        op.waits = {}
        op.dwaits = set()
        for d in deps:
            if d is op:
                continue
            if d.dma:
                op.dwaits.add(d)
            else:
                if d.st == "pe" and st == "pe" and not dma:
                    continue
                if op.waits.get(d.st, -1) < d.idx:
                    op.waits[d.st] = d.idx
        self.ops[st].append(op)
        return op

    def op(self, st, fn, reads=(), writes=()):
        return self._rec(st, False, fn, reads, writes)

    def dma(self, st, out, in_, reads=(), writes=(), **kw):
        return self._rec(st, True, lambda e: e.dma_start(out=out, in_=in_, **kw), reads, writes)

    def barrier(self):
        bt = self.t("__barrier__", len(self.tiles))
        lasts = []
        for s in self.STREAMS:
            if self.ops[s]:
                for o in reversed(self.ops[s]):
                    if not o.dma:
                        lasts.append(o)
                        break
                lasts.extend(self.dmaops[s][-NDSEM:])
        self._blasts = lasts
        for s in ("pe", "act", "dve", "pool", "sp"):
            op = self._rec(s, False, lambda e: e.nop(), (), ())
            for d in lasts:
                if d.dma:
                    op.dwaits.add(d)
                elif not (d.st == "pe" and s == "pe"):
                    if op.waits.get(d.st, -1) < d.idx:
                        op.waits[d.st] = d.idx

    def emit(self):
        nc = self.nc
        for s in self.STREAMS:
            for o in self.ops[s]:
                for ps, pidx in o.waits.items():
                    self.ops[ps][pidx].sig = True
        for s in self.STREAMS:
            c = 0
            for o in self.ops[s]:
                if (not o.dma) and o.sig:
                    c += 1
                    o.cnt = c
        import contextlib, os as _os
        if _os.environ.get("KSTATS"):
            for s_ in self.STREAMS:
                print("KSTATS", s_, "ops", len(self.ops[s_]), "signals", max([o.cnt for o in self.ops[s_]] + [0]), "dmas", self.ndma[s_], flush=True)
        with contextlib.ExitStack() as es:
            csem = {s: es.enter_context(nc.semaphore("c_" + s)) for s in self.STREAMS}
            dsem = {s: [es.enter_context(nc.semaphore("d_%s%d" % (s, i))) for i in range(NDSEM)]
                    for s in self.STREAMS if self.ndma[s]}
            block = es.enter_context(nc.Block())
            K_ = self

            def run(stream, eng):
                known = {s: 0 for s in K_.STREAMS}
                dknown = set()
                for o in K_.ops[stream]:
                    for ps, pidx in o.waits.items():
                        need = K_.ops[ps][pidx].cnt
                        if known[ps] < need:
                            eng.wait_ge(csem[ps], need)
                            known[ps] = need
                    for d in o.dwaits:
                        key = (d.st, d.didx)
                        if key in dknown:
                            continue
                        eng.wait_ge(dsem[d.st][d.didx % NDSEM], 16 * (d.didx // NDSEM + 1))
                        dknown.add(key)
                    ins = o.fn(eng)
                    if o.dma:
                        ins.then_inc(dsem[stream][o.didx % NDSEM], 16)
                    elif o.sig:
                        ins.then_inc(csem[stream], 1)

            @block.tensor
            def _(e):
                run("pe", e)

            @block.scalar
            def _(e):
                run("act", e)

            @block.vector
            def _(e):
                run("dve", e)

            @block.gpsimd
            def _(e):
                run("pool", e)

            @block.sync
            def _(e):
                run("sp", e)

import math

D = 1024
CTX = 256
NFM = 42
SCALE = 96 ** -0.5
EPS = 1e-6


class Ctx:
    pass


def _mm(k, out, lhsT, rhs, start, stop, R, W):
    k.op("pe", lambda e, o=out, l=lhsT, r=rhs, s=start, p=stop: e.matmul(o, l, r, start=s, stop=p), R, W)


def _tt(k, eng, out, a, b, op, R, W):
    k.op(eng, lambda e, o=out, a=a, b=b, op=op: e.tensor_tensor(o, a, b, op), R, W)


def _act(k, out, in_, func, R, W, bias=None, scale=None):
    kw = {}
    if bias is not None:
        kw["bias"] = bias
    if scale is not None:
        kw["scale"] = scale
    k.op("act", lambda e, o=out, i=in_, f=func, kw=kw: e.activation(o, i, f, **kw), R, W)


def _copy(k, eng, out, in_, R, W):
    if eng == "act":
        k.op("act", lambda e, o=out, i=in_: e.copy(o, i), R, W)
    else:
        k.op(eng, lambda e, o=out, i=in_: e.tensor_copy(o, i), R, W)


def build_program(T, NL, stop=None, dbg=()):
    NT = CTX + T
    nc = bass.Bass("TRN2", target_bir_lowering=False)
    k = K(nc)
    TT = k.t
    C = Ctx()
    C.nc, C.k, C.T, C.NT, C.NL = nc, k, T, NT, NL
    C.NTILE = NT // 128

    def din(name, shape, dt=F32):
        return nc.dram_tensor(name, list(shape), dt, kind="ExternalInput").ap()

    def dscr(name, shape, dt):
        return nc.dram_tensor(name, list(shape), dt, kind=("ExternalOutput" if name in dbg else "Internal")).ap()

    I = {}
    I["x"] = din("x", [T, D]); I["ctx"] = din("ctx", [CTX, D])
    I["cc"] = din("cc", [2, D])
    I["w_mod"] = din("w_mod", [NL, D, 6 * D]); I["b_mod"] = din("b_mod", [NL, 6 * D])
    I["norm1"] = din("norm1", [NL, D]); I["norm2"] = din("norm2", [NL, D])
    I["w_in"] = din("w_in", [NL, D, NFM * 128]); I["w_tm"] = din("w_tm", [NL, D, 768])
    I["qn"] = din("qn", [NL, 256]); I["kvn"] = din("kvn", [NL, 256])
    I["w_uq"] = din("w_uq", [NL, 256, 768]); I["w_uqs"] = din("w_uqs", [NL, 256, 768])
    I["w_ukv"] = din("w_ukv", [NL, 256, 1024]); I["w_o"] = din("w_o", [NL, 512, D])
    I["lam_re"] = din("lam_re", [NL, 2, 32, 64]); I["lam_im"] = din("lam_im", [NL, 2, 32, 64])
    I["log_dt"] = din("log_dt", [NL, 2, 32])
    I["b_re"] = din("b_re", [NL, 32, 64, 16]); I["b_im"] = din("b_im", [NL, 32, 64, 16])
    I["c_re"] = din("c_re", [NL, 32, 16, 64]); I["c_im"] = din("c_im", [NL, 32, 16, 64])
    I["ssm_d"] = din("ssm_d", [NL, 512]); I["w_glu"] = din("w_glu", [NL, 512, 512])
    I["b_glu"] = din("b_glu", [NL, 512]); I["s_wo"] = din("s_wo", [NL, 512, D])
    I["w_a2"] = din("w_a2", [NL, 2, 16, 256]); I["b_a2"] = din("b_a2", [NL, 2, 256])
    I["g_norm"] = din("g_norm", [NL, 512]); I["g_wo"] = din("g_wo", [NL, 512, D])
    I["w_out"] = din("w_out", [NL, D, D]); I["w_up"] = din("w_up", [NL, D, 5632])
    I["w_dn"] = din("w_dn", [NL, 2816, D]); I["fnorm"] = din("fnorm", [D])
    I["cosR"] = din("cosR", [32, T]); I["sinS"] = din("sinS", [32, T])
    I["cst"] = din("cst", [128, 1024])
    I["mk"] = din("mk", [128, 8])
    out = nc.dram_tensor("out", [T, D], F32, kind="ExternalOutput").ap()
    C.I, C.out = I, out

    S = {}
    S["xT"] = dscr("xT", [D, NT], F32)
    S["zT"] = dscr("zT", [NFM * 128, NT], BF16)
    S["QT"] = dscr("QT", [8, 96, NT], BF16); S["KT"] = dscr("KT", [8, 96, NT], BF16)
    S["V"] = dscr("Vs", [NT, 512], BF16)
    S["gkv"] = dscr("gkv", [NT, 768], BF16)
    S["attO"] = dscr("attO", [512, NT], BF16)
    S["yf"] = dscr("yf", [512, NT], F32); S["yb"] = dscr("yb", [512, NT], F32)
    S["s5A"] = dscr("s5A", [512, NT], BF16)
    S["of"] = dscr("of", [512, NT], F32); S["glaA"] = dscr("glaA", [512, NT], BF16)
    C.S = S

    C.ps = [nc.alloc_psum_tensor("ps%d" % i, [128, 512], F32) for i in range(8)]
    C.cst = k.sb([128, 1024], F32)
    C.cstb = k.sb([128, 1024], BF16)
    C.mk = k.sb([128, 8], F32)
    C.modT = k.sb([128, 48, 2], F32)
    C.vec = k.sb([128, 6, 8, 2], F32)
    C.epsT = k.sb([128, 1], F32)
    C.oneT = k.sb([128, 1], F32)
    C.hpiT = k.sb([128, 1], F32)
    C.onesb = k.sb([128, 128], BF16)
    C.onesf = k.sb([128, 128], F32)
    k.dma("sp", C.cst[:], I["cst"], writes=[TT("cst")])
    k.dma("pool", C.cstb[:], I["cst"], writes=[TT("cstb")])
    k.dma("sp", C.mk[:], I["mk"], writes=[TT("mk")])
    k.op("dve", lambda e: e.memset(C.epsT[:], EPS), [], [TT("eps")])
    k.op("dve", lambda e: e.memset(C.oneT[:], 1.0), [], [TT("one")])
    k.op("dve", lambda e: e.memset(C.hpiT[:], math.pi / 2), [], [TT("hpi")])
    k.op("dve", lambda e: e.memset(C.onesb[:], 1.0), [], [TT("onesb")])
    k.op("dve", lambda e: e.memset(C.onesf[:], 1.0), [], [TT("onesf")])
    k.mark()

    C.mts = [(0, CTX, True)] + [(CTX + i * 512, 512, False) for i in range(T // 512)]

    def _phases():
        yield "load", lambda: phase_load_x(C)
        for l in range(NL):
            last = (l == NL - 1)
            yield "mod%d" % l, lambda: phase_mod(C, l)
            yield "A%d" % l, lambda: phase_A(C, l)
            yield "qkv%d" % l, lambda: phase_qkv(C, l)
            yield "attn%d" % l, lambda: phase_attn(C, l)
            yield "s5%d" % l, lambda: phase_s5(C, l)
            yield "gla%d" % l, lambda: phase_gla(C, l)
            yield "E1%d" % l, lambda: phase_E1(C, l, last)
            yield "E2%d" % l, lambda: phase_E2(C, l, last)

    for name, fn in _phases():
        fn()
        if stop == name:
            break
    if "vec" in dbg:
        dv = nc.dram_tensor("vec", [128, 96], F32, kind="ExternalOutput").ap()
        k.dma("sp", dv, C.vec[:].rearrange("p a b c -> p (a b c)"), reads=[TT("vec")])
    k.barrier()
    k.emit()
    return nc


def mt_of_tile(ti):
    return 0 if ti < 2 else 1 + (ti - 2) // 4


def phase_load_x(C):
    k, nc, I, S = C.k, C.nc, C.I, C.S
    TT = k.t
    ident = C.cst[:, 0:128]
    xin = [k.sb([128, D], F32) for _ in range(2)]
    xo = [k.sb([128, 8, 128], F32) for _ in range(2)]
    for ti in range(C.NTILE):
        src = I["ctx"][ti * 128:(ti + 1) * 128, :] if ti < 2 else I["x"][(ti - 2) * 128:(ti - 1) * 128, :]
        b = ti % 2
        k.dma("sp", xin[b][:], src, writes=[TT("xin", b)])
        for kt in range(8):
            pb = kt
            k.op("pe", lambda e, o=C.ps[pb][:, 0:128], i=xin[b][:, kt * 128:(kt + 1) * 128]: e.transpose(o, i, ident),
                 [TT("xin", b), TT("cst")], [TT("ps", pb)])
            _copy(k, "act" if kt % 2 else "dve", xo[b][:, kt, :], C.ps[pb][:, 0:128], [TT("ps", pb)], [TT("xo", b)])
        k.dma("pool", S["xT"].rearrange("(kt p) t -> p kt t", p=128)[:, :, ti * 128:(ti + 1) * 128], xo[b][:],
              reads=[TT("xo", b)], writes=[TT("xT", mt_of_tile(ti))])
    k.barrier()
    k.reset()


def phase_mod(C, l):
    k, nc, I = C.k, C.nc, C.I
    TT = k.t
    cin = k.sb([128, 8, 2], F32)
    sc = k.sb([128, 8, 2], F32)
    for j in range(2):
        k.dma("sp", cin[:, :, j], I["cc"][j].rearrange("(kt p) -> p kt", p=128), writes=[TT("cin")], allow_slow_non_contiguous=True)
    _act(k, sc[:], cin[:], AF.Silu, [TT("cin")], [TT("sc")])
    bm = k.sb([128, 48], F32)
    k.dma("sp", bm[:], I["b_mod"][l].rearrange("(ft p) -> p ft", p=128), writes=[TT("bm")], allow_slow_non_contiguous=True)
    n12 = k.sb([128, 2, 8], F32)
    k.dma("sp", n12[:, 0, :], I["norm1"][l].rearrange("(ft p) -> p ft", p=128), writes=[TT("n12")], allow_slow_non_contiguous=True)
    k.dma("sp", n12[:, 1, :], I["norm2"][l].rearrange("(ft p) -> p ft", p=128), writes=[TT("n12")], allow_slow_non_contiguous=True)
    wm = [k.sb([128, 8, 1024], F32) for _ in range(2)]
    for ch in range(6):
        b = ch % 2
        k.dma("sp", wm[b][:], I["w_mod"][l].rearrange("(kt p) c -> p kt c", p=128)[:, :, ch * 1024:(ch + 1) * 1024],
              writes=[TT("wm", b)])
        for f in range(8):
            ft = ch * 8 + f
            pb = ft % 8
            for kt in range(8):
                _mm(k, C.ps[pb][:, 0:2], wm[b][:, kt, f * 128:(f + 1) * 128], sc[:, kt, :], kt == 0, kt == 7,
                    [TT("wm", b), TT("sc")], [TT("ps", pb)])
            k.op("dve", lambda e, o=C.modT[:, ft, :], i=C.ps[pb][:, 0:2], s=bm[:, ft:ft + 1]: e.tensor_scalar(o, i, s, None, ALU.add),
                 [TT("ps", pb), TT("bm")], [TT("modT")])
    m = C.modT
    for half in range(2):
        o0 = half * 24
        k.op("dve", lambda e, o=C.vec[:, 3 * half + 0, :, :], i=m[:, o0 + 8:o0 + 16, :], n=n12[:, half, :]:
             e.scalar_tensor_tensor(o, i, 1.0, n.unsqueeze(2).to_broadcast([128, 8, 2]), ALU.add, ALU.mult),
             [TT("modT"), TT("n12")], [TT("vec")])
        _copy(k, "dve", C.vec[:, 3 * half + 1, :, :], m[:, o0:o0 + 8, :], [TT("modT")], [TT("vec")])
        _copy(k, "dve", C.vec[:, 3 * half + 2, :, :], m[:, o0 + 16:o0 + 24, :], [TT("modT")], [TT("vec")])
    k.barrier()
    k.reset()


def colsum_rstd(C, sq_list, n, rs, denom, R, rs_tag, pb=7):
    k = C.k
    TT = k.t
    for i, sq in enumerate(sq_list):
        _mm(k, C.ps[pb][:, 0:n], C.onesb[:], sq, i == 0, i == len(sq_list) - 1, R + [TT("onesb")], [TT("ps", pb)])
    _act(k, rs[:, 0:n], C.ps[pb][:, 0:n], AF.Sqrt, [TT("ps", pb), TT("eps")], [rs_tag], bias=C.epsT[:, 0:1], scale=1.0 / denom)
    k.op("dve", lambda e: e.reciprocal(rs[:, 0:n], rs[:, 0:n]), [rs_tag], [rs_tag])


def norm_mod(C, xt, xtag, n, which, col, hT, sq, rs, tmp):
    k = C.k
    TT = k.t
    _act(k, sq[:, :, 0:n], xt[:, :, 0:n], AF.Square, [xtag], [TT("nm_sq")])
    colsum_rstd(C, [sq[:, kt, 0:n] for kt in range(8)], n, rs, float(D), [TT("nm_sq")], TT("nm_rs"))
    for kt in range(8):
        t = tmp[kt % 2]
        _tt(k, "dve", t[:, 0:n], xt[:, kt, 0:n], rs[:, 0:n], ALU.mult, [xtag, TT("nm_rs")], [TT("nm_tmp", kt % 2)])
        _act(k, hT[:, kt, 0:n], t[:, 0:n], AF.Identity, [TT("nm_tmp", kt % 2), TT("vec")], [TT("hT")],
             bias=C.vec[:, which + 1, kt, col:col + 1], scale=C.vec[:, which, kt, col:col + 1])


def load_w_bf(C, dst, src, tag, nsplit=1):
    k = C.k
    a = dst.shape[1]
    step = max(1, a // nsplit)
    for i in range(0, a, step):
        k.dma("pool", dst[:, i:i + step, :], src[:, i:i + step, :], writes=[tag])


def phase_A(C, l):
    k, nc, I, S = C.k, C.nc, C.I, C.S
    TT = k.t
    win = k.sb([128, 8, NFM * 128], BF16)
    wtm = k.sb([128, 8, 768], BF16)
    load_w_bf(C, win[:], I["w_in"][l].rearrange("(kt p) c -> p kt c", p=128), TT("win"), 8)
    load_w_bf(C, wtm[:], I["w_tm"][l].rearrange("(kt p) c -> p kt c", p=128), TT("wtm"), 2)
    xt = [k.sb([128, 8, 512], F32) for _ in range(2)]
    sq = k.sb([128, 8, 512], BF16)
    rs = k.sb([128, 512], F32)
    tmp = [k.sb([128, 512], F32) for _ in range(2)]
    hT = k.sb([128, 8, 512], BF16)
    stg = [k.sb([128, 512], BF16) for _ in range(6)]
    stm = [k.sb([128, 768], BF16) for _ in range(2)]
    nst = 0
    xTv = S["xT"].rearrange("(kt p) t -> p kt t", p=128)
    for mi, (c0, n, isctx) in enumerate(C.mts):
        b = mi % 2
        k.dma("sp", xt[b][:, :, 0:n], xTv[:, :, c0:c0 + n], reads=[TT("xT", mi)], writes=[TT("xtA", b)])
        norm_mod(C, xt[b], TT("xtA", b), n, 0, 1 if isctx else 0, hT, sq, rs, tmp)
        for ct in range(NFM):
            pb = ct % 6
            for kt in range(8):
                _mm(k, C.ps[pb][:, 0:n], win[:, kt, ct * 128:(ct + 1) * 128], hT[:, kt, 0:n], kt == 0, kt == 7,
                    [TT("win"), TT("hT")], [TT("ps", pb)])
            sb_ = nst % 6
            nst += 1
            if ct >= 18:
                _act(k, stg[sb_][:, 0:n], C.ps[pb][:, 0:n], AF.Sigmoid, [TT("ps", pb)], [TT("stg", sb_)])
            elif ct >= 14:
                _act(k, stg[sb_][:, 0:n], C.ps[pb][:, 0:n], AF.Silu, [TT("ps", pb)], [TT("stg", sb_)])
            else:
                _copy(k, "dve", stg[sb_][:, 0:n], C.ps[pb][:, 0:n], [TT("ps", pb)], [TT("stg", sb_)])
            k.dma("pool", S["zT"][ct * 128:(ct + 1) * 128, c0:c0 + n], stg[sb_][:, 0:n],
                  reads=[TT("stg", sb_)], writes=[TT("zT", ct, mi)])
        for j in range(n // 128):
            ti = c0 // 128 + j
            _ps = (6, 7)
            for kt in range(8):
                _mm(k, C.ps[6][:, 0:256], hT[:, kt, j * 128:(j + 1) * 128], wtm[:, kt, 0:256], kt == 0, kt == 7,
                    [TT("wtm"), TT("hT")], [TT("ps", 6)])
            for kt in range(8):
                _mm(k, C.ps[7][:, 0:512], hT[:, kt, j * 128:(j + 1) * 128], wtm[:, kt, 256:768], kt == 0, kt == 7,
                    [TT("wtm"), TT("hT")], [TT("ps", 7)])
            sb_ = ti % 2
            _copy(k, "dve", stm[sb_][:, 0:256], C.ps[6][:, 0:256], [TT("ps", 6)], [TT("stm", sb_)])
            _copy(k, "act", stm[sb_][:, 256:768], C.ps[7][:, 0:512], [TT("ps", 7)], [TT("stm", sb_)])
            k.dma("pool", S["gkv"][ti * 128:(ti + 1) * 128, :], stm[sb_][:], reads=[TT("stm", sb_)], writes=[TT("gkv", ti)])
    k.barrier()
    k.reset()


def phase_qkv(C, l):
    k, nc, I, S = C.k, C.nc, C.I, C.S
    TT = k.t
    wuq = k.sb([128, 2, 768], BF16); wuqs = k.sb([128, 2, 768], BF16); wukv = k.sb([128, 2, 1024], BF16)
    load_w_bf(C, wuq[:], I["w_uq"][l].rearrange("(kt p) c -> p kt c", p=128), TT("wuq"))
    load_w_bf(C, wuqs[:], I["w_uqs"][l].rearrange("(kt p) c -> p kt c", p=128), TT("wuqs"))
    load_w_bf(C, wukv[:], I["w_ukv"][l].rearrange("(kt p) c -> p kt c", p=128), TT("wukv"))
    gn = k.sb([128, 2, 2], F32)
    k.dma("sp", gn[:, 0, :], I["qn"][l].rearrange("(kt p) -> p kt", p=128), writes=[TT("gn")], allow_slow_non_contiguous=True)
    k.dma("sp", gn[:, 1, :], I["kvn"][l].rearrange("(kt p) -> p kt", p=128), writes=[TT("gn")], allow_slow_non_contiguous=True)
    zq = k.sb([128, 2, 512], BF16); zkv = k.sb([128, 2, 512], BF16)
    m1 = k.sb([128, 512], BF16); m2 = k.sb([128, 512], BF16)
    sqq = k.sb([128, 2, 512], BF16); sqk = k.sb([128, 2, 512], BF16)
    cqg = k.sb([128, 2, 512], BF16); ckg = k.sb([128, 2, 512], BF16)
    rq = k.sb([128, 512], F32); rk = k.sb([128, 512], F32)
    cosT = k.sb([128, 512], F32); sinT = k.sb([128, 512], F32)
    cr = k.sb([128, 512], F32); sr = k.sb([128, 512], F32)
    t1 = k.sb([128, 512], F32); t2 = k.sb([128, 512], F32); krr = k.sb([128, 512], F32)
    QTt = k.sb([128, 8, 512], BF16); KTt = k.sb([128, 8, 512], BF16)
    rkt = k.sb([128, 4], F32)
    Vt = [k.sb([128, 512], BF16) for _ in range(2)]
    zTv = S["zT"]
    R6 = slice(64, 96)
    for mi, (c0, n, isctx) in enumerate(C.mts):
        k.dma("sp", zq[:, :, 0:n], zTv[0:256, c0:c0 + n].rearrange("(kt p) t -> p kt t", p=128),
              reads=[TT("zT", 0, mi), TT("zT", 1, mi)], writes=[TT("zq")])
        k.dma("sp", zkv[:, :, 0:n], zTv[256:512, c0:c0 + n].rearrange("(kt p) t -> p kt t", p=128),
              reads=[TT("zT", 2, mi), TT("zT", 3, mi)], writes=[TT("zkv")])
        k.dma("sp", m1[:, 0:n], zTv[512:640, c0:c0 + n], reads=[TT("zT", 4, mi)], writes=[TT("m1")])
        k.dma("sp", m2[:, 0:n], zTv[640:768, c0:c0 + n], reads=[TT("zT", 5, mi)], writes=[TT("m2")])
        if not isctx:
            k.dma("sp", cosT[R6, 0:n], I["cosR"][:, c0 - CTX:c0 - CTX + n], writes=[TT("cosT")])
            k.dma("sp", sinT[R6, 0:n], I["sinS"][:, c0 - CTX:c0 - CTX + n], writes=[TT("sinT")])
        _act(k, sqq[:, :, 0:n], zq[:, :, 0:n], AF.Square, [TT("zq")], [TT("sqq")])
        _act(k, sqk[:, :, 0:n], zkv[:, :, 0:n], AF.Square, [TT("zkv")], [TT("sqk")])
        colsum_rstd(C, [sqq[:, kt, 0:n] for kt in range(2)], n, rq, 256.0, [TT("sqq")], TT("rq"), pb=6)
        colsum_rstd(C, [sqk[:, kt, 0:n] for kt in range(2)], n, rk, 256.0, [TT("sqk")], TT("rk"), pb=7)
        for kt in range(2):
            k.op("dve", lambda e, kt=kt, n=n: e.tensor_scalar(cqg[:, kt, 0:n], zq[:, kt, 0:n], gn[:, 0, kt:kt + 1], None, ALU.mult),
                 [TT("zq"), TT("gn")], [TT("cqg")])
            k.op("pool", lambda e, kt=kt, n=n: e.tensor_scalar(ckg[:, kt, 0:n], zkv[:, kt, 0:n], gn[:, 1, kt:kt + 1], None, ALU.mult),
                 [TT("zkv"), TT("gn")], [TT("ckg")])
        if not isctx:
            _tt(k, "dve", cr[R6, 0:n], cosT[R6, 0:n], rq[R6, 0:n], ALU.mult, [TT("cosT"), TT("rq")], [TT("cr")])
            _tt(k, "dve", sr[R6, 0:n], sinT[R6, 0:n], rq[R6, 0:n], ALU.mult, [TT("sinT"), TT("rq")], [TT("sr")])
        for h in range(8):
            pa, pb = (2 * h) % 6, (2 * h + 1) % 6
            for kt in range(2):
                _mm(k, C.ps[pa][0:96, 0:n], wuq[:, kt, h * 96:(h + 1) * 96], cqg[:, kt, 0:n], kt == 0, kt == 1,
                    [TT("wuq"), TT("cqg")], [TT("ps", pa)])
            _tt(k, "dve", QTt[0:64, h, 0:n], C.ps[pa][0:64, 0:n], rq[0:64, 0:n], ALU.mult, [TT("ps", pa), TT("rq")], [TT("QTt")])
            if isctx:
                _tt(k, "dve", QTt[R6, h, 0:n], C.ps[pa][R6, 0:n], rq[R6, 0:n], ALU.mult, [TT("ps", pa), TT("rq")], [TT("QTt")])
            else:
                for kt in range(2):
                    _mm(k, C.ps[pb][0:96, 0:n], wuqs[:, kt, h * 96:(h + 1) * 96], cqg[:, kt, 0:n], kt == 0, kt == 1,
                        [TT("wuqs"), TT("cqg")], [TT("ps", pb)])
                _tt(k, "dve", t1[R6, 0:n], C.ps[pa][R6, 0:n], cr[R6, 0:n], ALU.mult, [TT("ps", pa), TT("cr")], [TT("t1")])
                _tt(k, "dve", t2[R6, 0:n], C.ps[pb][R6, 0:n], sr[R6, 0:n], ALU.mult, [TT("ps", pb), TT("sr")], [TT("t2")])
                _tt(k, "pool", QTt[R6, h, 0:n], t1[R6, 0:n], t2[R6, 0:n], ALU.add, [TT("t1"), TT("t2")], [TT("QTt")])
        k.dma("pool", S["QT"][:, :, c0:c0 + n].rearrange("h r t -> r h t"), QTt[0:96, :, 0:n], reads=[TT("QTt")], writes=[TT("QT", mi)])
        for h in range(8):
            pa = h % 6
            for kt in range(2):
                _mm(k, C.ps[pa][0:64, 0:n], wukv[:, kt, h * 128:h * 128 + 64], ckg[:, kt, 0:n], kt == 0, kt == 1,
                    [TT("wukv"), TT("ckg")], [TT("ps", pa)])
            _tt(k, "dve", KTt[0:64, h, 0:n], C.ps[pa][0:64, 0:n], rk[0:64, 0:n], ALU.mult, [TT("ps", pa), TT("rk")], [TT("KTt")])
        if isctx:
            _copy(k, "dve", krr[R6, 0:n], m1[R6, 0:n], [TT("m1")], [TT("krr")])
        else:
            _tt(k, "dve", t1[R6, 0:n], m1[R6, 0:n], cosT[R6, 0:n], ALU.mult, [TT("m1"), TT("cosT")], [TT("t1")])
            _tt(k, "dve", t2[R6, 0:n], m2[R6, 0:n], sinT[R6, 0:n], ALU.mult, [TT("m2"), TT("sinT")], [TT("t2")])
            _tt(k, "pool", krr[R6, 0:n], t1[R6, 0:n], t2[R6, 0:n], ALU.add, [TT("t1"), TT("t2")], [TT("krr")])
        _copy(k, "pool", KTt[R6, :, 0:n], krr[R6, 0:n].unsqueeze(1).to_broadcast([32, 8, n]), [TT("krr")], [TT("KTt")])
        k.dma("pool", S["KT"][:, :, c0:c0 + n].rearrange("h r t -> r h t"), KTt[0:96, :, 0:n], reads=[TT("KTt")], writes=[TT("KT", mi)])
        wv = wukv[:].rearrange("p k (h c) -> p k h c", c=128)
        for j in range(n // 128):
            ti = c0 // 128 + j
            js = slice(j * 128, (j + 1) * 128)
            for kt in range(2):
                _mm(k, C.ps[6][:, 0:1], sqk[:, kt, js], C.onesb[:, 0:1], kt == 0, kt == 1, [TT("sqk"), TT("onesb")], [TT("ps", 6)])
            _act(k, rkt[:, j:j + 1], C.ps[6][:, 0:1], AF.Sqrt, [TT("ps", 6), TT("eps")], [TT("rkt")], bias=C.epsT[:, 0:1], scale=1.0 / 256)
            k.op("dve", lambda e, j=j: e.reciprocal(rkt[:, j:j + 1], rkt[:, j:j + 1]), [TT("rkt")], [TT("rkt")])
            pv = 7
            for kt in range(2):
                _mm(k, C.ps[pv][:, 0:512].rearrange("p (h c) -> p h c", c=64), ckg[:, kt, js], wv[:, kt, :, 64:128], kt == 0, kt == 1,
                    [TT("wukv"), TT("ckg")], [TT("ps", pv)])
            vb = ti % 2
            k.op("dve", lambda e, j=j, vb=vb: e.tensor_scalar(Vt[vb][:], C.ps[pv][:, 0:512], rkt[:, j:j + 1], None, ALU.mult),
                 [TT("ps", pv), TT("rkt")], [TT("Vt", vb)])
            k.dma("pool", S["V"][ti * 128:(ti + 1) * 128, :], Vt[vb][:], reads=[TT("Vt", vb)], writes=[TT("V", ti)])
    k.barrier()
    k.reset()


def phase_attn(C, l):
    k, nc, I, S = C.k, C.nc, C.I, C.S
    TT = k.t
    NT, NTILE = C.NT, C.NTILE
    KTh = [k.sb([128, NT], BF16) for _ in range(2)]
    QTh = [k.sb([128, NT], BF16) for _ in range(2)]
    Vh = [k.sb([128, NTILE, 65], BF16) for _ in range(2)]
    P = [k.sb([128, 512], BF16) for _ in range(5)]
    RING = (0, 1, 2, 3, 7)
    rden = k.sb([128, 512], F32)
    bc = k.sb([128, 512], F32)
    on = [k.sb([128, 512], BF16) for _ in range(2)]
    for b in range(2):
        k.op("dve", lambda e, b=b: e.memset(Vh[b][:, :, 64:65], 1.0), [], [TT("Vh", b)])
    allmt = list(range(len(C.mts)))
    items = []
    for h in range(8):
        for mi, (c0, n, isctx) in enumerate(C.mts):
            nkt = 2 if isctx else NTILE
            for kt in range(nkt):
                items.append((h, mi, kt, nkt))
    LA = 4
    qidx = {}
    for it_ in items:
        key = (it_[0], it_[1])
        if key not in qidx:
            qidx[key] = len(qidx)
    loaded = set()

    def issue_S(i):
        h, mi, kt, nkt = items[i]
        c0, n, isctx = C.mts[mi]
        b = h % 2
        if h not in loaded:
            loaded.add(h)
            k.dma("sp", KTh[b][0:96, :], S["KT"][h], reads=[TT("KT", m_) for m_ in allmt], writes=[TT("KTh", b)])
            k.dma("sp", QTh[b][0:96, :], S["QT"][h], reads=[TT("QT", m_) for m_ in allmt], writes=[TT("QTh", b)])
            k.dma("sp", Vh[b][:, :, 0:64], S["V"].rearrange("(n p) (h d) -> p n h d", p=128, d=64)[:, :, h, :],
                  reads=[TT("V", ti) for ti in range(NTILE)], writes=[TT("Vh", b)])
        pidx = i % 5
        pscr = RING[pidx]
        _mm(k, C.ps[pscr][:, 0:n], KTh[b][0:96, kt * 128:(kt + 1) * 128], QTh[b][0:96, c0:c0 + n], True, True,
            [TT("KTh", b), TT("QTh", b)], [TT("ps", pscr)])
        _act(k, P[pidx][:, 0:n], C.ps[pscr][:, 0:n], AF.Exp, [TT("ps", pscr)], [TT("P", pidx)], scale=SCALE)

    def issue_PV(i):
        h, mi, kt, nkt = items[i]
        c0, n, isctx = C.mts[mi]
        b = h % 2
        pidx = i % 5
        pscr = RING[pidx]
        nq = qidx[(h, mi)]
        po = 4 + nq % 2
        _mm(k, C.ps[po][0:65, 0:n], Vh[b][:, kt, :], P[pidx][:, 0:n], kt == 0, kt == nkt - 1,
            [TT("Vh", b), TT("P", pidx)], [TT("ps", po)])
        if kt == nkt - 1:
            k.op("dve", lambda e, po=po, n=n: e.reciprocal(rden[64:65, 0:n], C.ps[po][64:65, 0:n]), [TT("ps", po)], [TT("rden")])
            _mm(k, C.ps[6][0:64, 0:n], C.onesf[64:65, 0:64], rden[64:65, 0:n], True, True, [TT("rden"), TT("onesf")], [TT("ps", 6)])
            _copy(k, "act", bc[0:64, 0:n], C.ps[6][0:64, 0:n], [TT("ps", 6)], [TT("bc")])
            ob = nq % 2
            _tt(k, "dve", on[ob][0:64, 0:n], C.ps[po][0:64, 0:n], bc[0:64, 0:n], ALU.mult, [TT("ps", po), TT("bc")], [TT("on", ob)])
            k.dma("pool", S["attO"][h * 64:(h + 1) * 64, c0:c0 + n], on[ob][0:64, 0:n], reads=[TT("on", ob)], writes=[TT("attO", mi)])

    for i in range(len(items) + LA):
        if i < len(items):
            issue_S(i)
        if i >= LA:
            issue_PV(i - LA)
    k.barrier()
    k.reset()


def phase_s5(C, l):
    k, nc, I, S = C.k, C.nc, C.I, C.S
    TT = k.t
    NTILE = C.NTILE
    P_ = [TT("s5p")]

    def ew(out, a, b, op, eng="dve"):
        _tt(k, eng, out, a, b, op, P_, P_)

    def ts(out, a, s1, s2, op0, op1=None):
        if op1 is None:
            k.op("dve", lambda e: e.tensor_scalar(out, a, s1, None, op0), P_, P_)
        else:
            k.op("dve", lambda e: e.tensor_scalar(out, a, s1, s2, op0, op1), P_, P_)

    def disc(lr, li, ld, shp):
        names = ["dt", "a", "mag", "ang", "s", "c", "t1", "t2", "ar", "ai", "den", "am1", "fr", "fi"]
        t = {nm: k.sb([128] + shp, F32) for nm in names}
        A = lambda x: x[:]
        _act(k, A(t["dt"]), A(ld), AF.Exp, P_, P_)
        ew(A(t["a"]), A(t["dt"]), A(lr), ALU.mult)
        _act(k, A(t["mag"]), A(t["a"]), AF.Exp, P_, P_)
        ew(A(t["ang"]), A(t["dt"]), A(li), ALU.mult)
        _act(k, A(t["s"]), A(t["ang"]), AF.Sin, P_, P_, scale=1.0 / 16)
        _act(k, A(t["c"]), A(t["ang"]), AF.Sin, P_ + [TT("hpi")], P_, bias=C.hpiT[:, 0:1], scale=1.0 / 16)
        for _ in range(4):
            ew(A(t["t1"]), A(t["c"]), A(t["c"]), ALU.mult)
            ew(A(t["t2"]), A(t["s"]), A(t["s"]), ALU.mult)
            ew(A(t["s"]), A(t["s"]), A(t["c"]), ALU.mult)
            ts(A(t["s"]), A(t["s"]), 2.0, None, ALU.mult)
            ew(A(t["c"]), A(t["t1"]), A(t["t2"]), ALU.subtract)
        ew(A(t["ar"]), A(t["mag"]), A(t["c"]), ALU.mult)
        ew(A(t["ai"]), A(t["mag"]), A(t["s"]), ALU.mult)
        ew(A(t["t1"]), A(lr), A(lr), ALU.mult)
        ew(A(t["t2"]), A(li), A(li), ALU.mult)
        ew(A(t["den"]), A(t["t1"]), A(t["t2"]), ALU.add)
        k.op("dve", lambda e: e.reciprocal(A(t["den"]), A(t["den"])), P_, P_)
        ts(A(t["am1"]), A(t["ar"]), -1.0, None, ALU.add)
        ew(A(t["t1"]), A(t["am1"]), A(lr), ALU.mult)
        ew(A(t["t2"]), A(t["ai"]), A(li), ALU.mult)
        ew(A(t["t1"]), A(t["t1"]), A(t["t2"]), ALU.add)
        ew(A(t["fr"]), A(t["t1"]), A(t["den"]), ALU.mult)
        ew(A(t["t1"]), A(t["ai"]), A(lr), ALU.mult)
        ew(A(t["t2"]), A(t["am1"]), A(li), ALU.mult)
        ew(A(t["t1"]), A(t["t1"]), A(t["t2"]), ALU.subtract)
        ew(A(t["fi"]), A(t["t1"]), A(t["den"]), ALU.mult)
        return t

    CS = k.sb([128, 2, 2, 16, 128], F32)
    RT = k.sb([128, 32, 128], F32)
    Ctab = CS[:, :, 0]
    Stab = CS[:, :, 1]
    XL = k.sb([128, 2, 2, 4, 4, 2, 64], BF16)
    RL = k.sb([128, 2, 16, 128], BF16)
    dsk = k.sb([128, 4], F32)
    k.dma("sp", dsk[:], I["ssm_d"][l].rearrange("(ft p) -> p ft", p=128), writes=[TT("dsk")], allow_slow_non_contiguous=True)
    mark2 = k.sb_off
    lrA = k.sb([128, 32], F32); liA = k.sb([128, 32], F32); ldA = k.sb([128, 32], F32)
    for d_ in range(2):
        k.dma("sp", lrA[:, d_ * 16:(d_ + 1) * 16], I["lam_re"][l, d_].rearrange("(gp g2) p -> (g2 p) gp", g2=2),
              writes=P_, allow_slow_non_contiguous=True)
        k.dma("sp", liA[:, d_ * 16:(d_ + 1) * 16], I["lam_im"][l, d_].rearrange("(gp g2) p -> (g2 p) gp", g2=2),
              writes=P_, allow_slow_non_contiguous=True)
    for g2 in range(2):
        k.dma("sp", ldA[g2 * 64:(g2 + 1) * 64, :].rearrange("p (d g) -> p d g", d=2),
              I["log_dt"][l].rearrange("d (gp g2) -> g2 d gp", g2=2)[g2].partition_broadcast(64),
              writes=P_, allow_slow_non_contiguous=True)
    dA = disc(lrA, liA, ldA, [32])
    magA = k.sb([128, 32], F32)
    _copy(k, "dve", magA[:], dA["mag"][:], P_, P_)
    pc = k.sb([128, 32], F32); psn = k.sb([128, 32], F32); q1 = k.sb([128, 32], F32); q2 = k.sb([128, 32], F32)
    tA = k.sb([128, 32, 64], F32); tB = k.sb([128, 32, 64], F32)
    _copy(k, "dve", pc[:], dA["c"][:], P_, P_)
    _copy(k, "dve", psn[:], dA["s"][:], P_, P_)
    v3 = lambda t_: t_.rearrange("p (d g) -> p d g", d=2)
    _copy(k, "dve", Ctab[:, :, :, 0], v3(pc[:]), P_, P_ + [TT("s5tab")])
    _copy(k, "dve", Stab[:, :, :, 0], v3(psn[:]), P_, P_ + [TT("s5tab")])
    tA4 = tA[:].rearrange("p (d g) t -> p d g t", d=2)
    tB4 = tB[:].rearrange("p (d g) t -> p d g t", d=2)
    blk = 1
    while blk < 128:
        pcb = v3(pc[:]).unsqueeze(3).to_broadcast([128, 2, 16, blk])
        psb = v3(psn[:]).unsqueeze(3).to_broadcast([128, 2, 16, blk])
        C0, S0 = Ctab[:, :, :, 0:blk], Stab[:, :, :, 0:blk]
        ew(tA4[:, :, :, 0:blk], C0, pcb, ALU.mult)
        ew(tB4[:, :, :, 0:blk], S0, psb, ALU.mult)
        ew(Ctab[:, :, :, blk:2 * blk], tA4[:, :, :, 0:blk], tB4[:, :, :, 0:blk], ALU.subtract)
        ew(tA4[:, :, :, 0:blk], C0, psb, ALU.mult)
        ew(tB4[:, :, :, 0:blk], S0, pcb, ALU.mult)
        ew(Stab[:, :, :, blk:2 * blk], tA4[:, :, :, 0:blk], tB4[:, :, :, 0:blk], ALU.add)
        ew(q1[:], pc[:], pc[:], ALU.mult)
        ew(q2[:], psn[:], psn[:], ALU.mult)
        ew(psn[:], psn[:], pc[:], ALU.mult)
        ts(psn[:], psn[:], 2.0, None, ALU.mult)
        ew(pc[:], q1[:], q2[:], ALU.subtract)
        blk *= 2
    CrA = k.sb([128, 16, 16], F32); CiA = k.sb([128, 16, 16], F32)
    for g2 in range(2):
        for gp_ in range(16):
            k.dma("sp", CrA[g2 * 64:(g2 + 1) * 64, gp_, :], I["c_re"][l, 2 * gp_ + g2].rearrange("h p -> p h"), writes=P_, allow_slow_non_contiguous=True)
            k.dma("sp", CiA[g2 * 64:(g2 + 1) * 64, gp_, :], I["c_im"][l, 2 * gp_ + g2].rearrange("h p -> p h"), writes=P_, allow_slow_non_contiguous=True)
    k.op("dve", lambda e: e.memset(RL[:], 0.0), P_, P_)
    for gq in range(4):
        for g2 in range(2):
            ps_ = slice(g2 * 64, (g2 + 1) * 64)
            cs_ = slice(gq * 32 + g2 * 16, gq * 32 + g2 * 16 + 16)
            _copy(k, "dve", RL[ps_, 0, gq::4, cs_], CrA[ps_, gq::4, :], P_, P_)
            k.op("dve", lambda e, o=RL[ps_, 1, gq::4, cs_], i=CiA[ps_, gq::4, :]: e.tensor_scalar(o, i, -1.0, None, ALU.mult), P_, P_)
    lrB = k.sb([128, 2, 4, 64], F32); liB = k.sb([128, 2, 4, 64], F32); ldB = k.sb([128, 2, 4, 64], F32)
    ldb0 = k.sb([128, 2, 4], F32)
    brB = k.sb([128, 4, 64], F32); biB = k.sb([128, 4, 64], F32)
    for q8 in range(8):
        ps_ = slice(q8 * 16, (q8 + 1) * 16)
        k.dma("sp", lrB[ps_], I["lam_re"][l][:, q8::8, :].partition_broadcast(16), writes=P_)
        k.dma("sp", liB[ps_], I["lam_im"][l][:, q8::8, :].partition_broadcast(16), writes=P_)
        k.dma("sp", ldb0[ps_], I["log_dt"][l][:, q8::8].partition_broadcast(16), writes=P_, allow_slow_non_contiguous=True)
        for ft_ in range(4):
            k.dma("sp", brB[ps_, ft_, :], I["b_re"][l, ft_ * 8 + q8].rearrange("p h -> h p"), writes=P_, allow_slow_non_contiguous=True)
            k.dma("sp", biB[ps_, ft_, :], I["b_im"][l, ft_ * 8 + q8].rearrange("p h -> h p"), writes=P_, allow_slow_non_contiguous=True)
    _copy(k, "dve", ldB[:], ldb0[:].unsqueeze(3).to_broadcast([128, 2, 4, 64]), P_, P_)
    dB = disc(lrB, liB, ldB, [2, 4, 64])
    bbr = k.sb([128, 2, 4, 64], F32); bbi = k.sb([128, 2, 4, 64], F32)
    u1 = k.sb([128, 2, 4, 64], F32); u2 = k.sb([128, 2, 4, 64], F32)
    brb = brB[:].unsqueeze(1).to_broadcast([128, 2, 4, 64]); bib = biB[:].unsqueeze(1).to_broadcast([128, 2, 4, 64])
    ew(u1[:], dB["fr"][:], brb, ALU.mult); ew(u2[:], dB["fi"][:], bib, ALU.mult); ew(bbr[:], u1[:], u2[:], ALU.subtract)
    ew(u1[:], dB["fr"][:], bib, ALU.mult); ew(u2[:], dB["fi"][:], brb, ALU.mult); ew(bbi[:], u1[:], u2[:], ALU.add)
    mkb = C.mk[:, 0:8].rearrange("p (a b) -> p a b", b=2).unsqueeze(3).to_broadcast([128, 4, 2, 64])
    for d in range(2):
        for ri, bb in enumerate((bbr, bbi)):
            for ft in range(4):
                src = bb[:, d, ft, :].unsqueeze(1).unsqueeze(1).to_broadcast([128, 4, 2, 64])
                k.op("dve", lambda e, o=XL[:, d, ri, ft], s=src: e.tensor_tensor(o, s, mkb, ALU.mult), P_ + [TT("mk")], P_)
    k.barrier()
    k.sb_off = mark2
    magA2 = k.sb([128, 32], F32)
    _copy(k, "dve", magA2[:], dA["mag"][:], [], [TT("magA2")])
    k.barrier()
    _copy(k, "dve", RT[:], magA2[:].unsqueeze(2).to_broadcast([128, 32, 128]), [TT("magA2")], [TT("s5RT")])
    k.op("dve", lambda e: e.memset(RT[:, :, 0:1], 0.0), [TT("s5RT")], [TT("s5RT")])

    X = [k.sb([128, 2, 16, 128], F32) for _ in range(2)]
    TA = [k.sb([128, 2, 16, 128], F32) for _ in range(2)]
    TB = [k.sb([128, 2, 16, 128], F32) for _ in range(2)]
    Hb = [k.sb([128, 2, 16, 128], BF16) for _ in range(2)]
    cry = [k.sb([128, 2, 16], F32) for _ in range(2)]
    c1 = [k.sb([128, 16], F32) for _ in range(2)]
    c2 = [k.sb([128, 16], F32) for _ in range(2)]
    c3 = [k.sb([128, 2, 16], F32) for _ in range(2)]
    CSl = [k.sb([128, 2, 16], F32) for _ in range(2)]
    for d_ in range(2):
        _tt(k, "dve", CSl[d_][:], CS[:, d_, :, :, 127], magA2[:, d_ * 16:(d_ + 1) * 16].unsqueeze(1).to_broadcast([128, 2, 16]), ALU.mult,
            [TT("s5tab"), TT("magA2")], [TT("s5CSl")])
    ub = [k.sb([128, 4, 128], BF16) for _ in range(2)]
    ysb = [k.sb([128, 4, 128], F32) for _ in range(2)]
    chain = [list(range(NTILE)), [1, 0] + list(range(NTILE - 1, 1, -1))]
    for d in range(2):
        k.op("dve" if d == 0 else "pool", lambda e, d=d: e.memset(cry[d][:], 0.0), [], [TT("s5c", d)])
    def st_x(it, d):
        eng = "dve" if d == 0 else "pool"
        ti = chain[d][it]
        mi = mt_of_tile(ti)
        cs = slice(ti * 128, (ti + 1) * 128)
        k.dma("sp", ub[d][:], S["zT"][6 * 128:10 * 128, cs].rearrange("(ft p) t -> p ft t", p=128),
              reads=[TT("zT", 6 + ft, mi) for ft in range(4)], writes=[TT("s5u", d)])
        for j in range(8):
            pb = d * 3 + (j % 3)
            for gpl in range(2):
                gp = 2 * j + gpl
                for ri in range(2):
                    slot = ri * 2 + gpl
                    lhsT = XL[:, d, ri, gp // 4, gp % 4].rearrange("p a b -> p (a b)")
                    _mm(k, C.ps[pb][:, slot * 128:(slot + 1) * 128], lhsT, ub[d][:, gp // 4, :], True, True,
                        [TT("s5p"), TT("s5u", d)], [TT("ps", pb)])
            dst = X[d][:, :, 2 * j:2 * j + 2, :] if d == 0 else X[d][:, :, 2 * j:2 * j + 2, ::-1]
            _copy(k, "act", dst, C.ps[pb][:, 0:512].rearrange("p (r g t) -> p r g t", r=2, g=2), [TT("ps", pb)], [TT("s5X", d, j)])

    def st_rot(it, d):
        eng = "dve" if d == 0 else "pool"
        ti = chain[d][it]
        mi = mt_of_tile(ti)
        cs = slice(ti * 128, (ti + 1) * 128)
        CSd = CS[:, d]
        allX = [TT("s5X", d, j) for j in range(8)]
        allA = [TT("s5A_", d, ri, gp) for ri in range(2) for gp in range(16)]
        tB_, tH, tc = TT("s5B", d), TT("s5H", d), TT("s5c", d)
        tab = TT("s5tab")
        Xd, A_, B_ = X[d], TA[d], TB[d]
        _tt(k, eng, A_[:], Xd[:], CSd, ALU.mult, allX + [tab], allA)
        _tt(k, eng, B_[:], Xd[:, ::-1], CSd, ALU.mult, allX + [tab], [tB_])
        _tt(k, eng, Xd[:, 0], A_[:, 0], A_[:, 1], ALU.add, allA, allX)
        _tt(k, eng, Xd[:, 1], B_[:, 0], B_[:, 1], ALU.subtract, [tB_], allX)

    def st_rest(it, d):
        eng = "dve" if d == 0 else "pool"
        ti = chain[d][it]
        mi = mt_of_tile(ti)
        cs = slice(ti * 128, (ti + 1) * 128)
        CSd = CS[:, d]
        allX = [TT("s5X", d, j) for j in range(8)]
        allA = [TT("s5A_", d, ri, gp) for ri in range(2) for gp in range(16)]
        tB_, tH, tc = TT("s5B", d), TT("s5H", d), TT("s5c", d)
        tab = TT("s5tab")
        Xd, A_, B_ = X[d], TA[d], TB[d]
        _tt(k, "dve", Xd[:, :, :, 0], Xd[:, :, :, 0], cry[d][:], ALU.add, allX + [tc], allX)
        for ri in range(2):
            k.op("dve", lambda e, o=A_[:, ri].rearrange("p g t -> p (g t)"), a=RT[:, d * 16:(d + 1) * 16, :].rearrange("p g t -> p (g t)"),
                 w=Xd[:, ri].rearrange("p g t -> p (g t)"): e.tensor_tensor_scan(o, a, w, 0.0, ALU.mult, ALU.add),
                 allX + [TT("s5RT")], [TT("s5A_", d, ri, gp) for gp in range(16)])
        GL = A_[:, :, :, 127]
        GLs = A_[:, ::-1, :, 127]
        _tt(k, "dve", c3[d][:], GL, CSl[d][:], ALU.mult, allA + [TT("s5CSl")], [TT("s5c1", d)])
        _tt(k, "dve", cry[d][:, 0, :], c3[d][:, 0, :], c3[d][:, 1, :], ALU.subtract, [TT("s5c1", d)], [tc])
        _tt(k, "dve", c3[d][:], GLs, CSl[d][:], ALU.mult, allA + [TT("s5CSl"), TT("s5c1", d)], [TT("s5c1", d)])
        _tt(k, "dve", cry[d][:, 1, :], c3[d][:, 0, :], c3[d][:, 1, :], ALU.add, [TT("s5c1", d)], [tc])
        HR = Hb[d][:, 0] if d == 0 else Hb[d][:, 0, :, ::-1]
        HI = Hb[d][:, 1] if d == 0 else Hb[d][:, 1, :, ::-1]
        _tt(k, "dve", B_[:], A_[:], CSd, ALU.mult, allA + [tab], [tB_])
        _tt(k, "dve", Xd[:], A_[:, ::-1], CSd, ALU.mult, allA + [tab], allX)
        _tt(k, "dve", HR, B_[:, 0], B_[:, 1], ALU.subtract, [tB_], [tH])
        _tt(k, "dve", HI, Xd[:, 0], Xd[:, 1], ALU.add, allX, [tH])
        pb = 6 + d
        for ft in range(4):
            n_ = 0
            for gq in range(4):
                gp = ft * 4 + gq
                for ri in range(2):
                    _mm(k, C.ps[pb][:, ft * 128:(ft + 1) * 128], RL[:, ri, gp, :], Hb[d][:, ri, gp, :], n_ == 0, n_ == 7,
                        [TT("s5p"), tH], [TT("ps", pb)])
                    n_ += 1
        _copy(k, "act", ysb[d][:], C.ps[pb][:, 0:512].rearrange("p (f t) -> p f t", f=4), [TT("ps", pb)], [TT("s5y", d)])
        dstD = S["yf"] if d == 0 else S["yb"]
        k.dma("act", dstD[:, cs].rearrange("(ft p) t -> p ft t", p=128), ysb[d][:], reads=[TT("s5y", d)],
              writes=[TT("yfb", d, ti)])


    st_x(0, 0)
    st_x(0, 1)
    st_rot(0, 1)
    for it in range(NTILE):
        st_rot(it, 0)
        st_rest(it, 0)
        if it + 1 < NTILE:
            st_x(it + 1, 0)
        st_rest(it, 1)
        if it + 1 < NTILE:
            st_x(it + 1, 1)
            st_rot(it + 1, 1)
    k.barrier()
    k.reset()
    wglu = k.sb([128, 4, 512], BF16)
    load_w_bf(C, wglu[:], I["w_glu"][l].rearrange("(kt p) c -> p kt c", p=128), TT("wglu"))
    bgl = k.sb([128, 4], F32); dsk2 = k.sb([128, 4], F32)
    k.dma("sp", bgl[:], I["b_glu"][l].rearrange("(ft p) -> p ft", p=128), writes=[TT("bgl")], allow_slow_non_contiguous=True)
    k.dma("sp", dsk2[:], I["ssm_d"][l].rearrange("(ft p) -> p ft", p=128), writes=[TT("dsk2")], allow_slow_non_contiguous=True)
    yf = k.sb([128, 4, 512], F32); yb = k.sb([128, 4, 512], F32); uu = k.sb([128, 4, 512], BF16)
    x2 = k.sb([128, 4, 512], F32); sg = k.sb([128, 4, 512], F32); g = k.sb([128, 4, 512], BF16)
    s2 = [k.sb([128, 512], F32) for _ in range(2)]
    Ao = k.sb([128, 4, 512], BF16)
    c0_ = 2.0 * math.sqrt(2.0 / math.pi)
    c1_ = c0_ * 0.044715
    for mi, (c0, n, isctx) in enumerate(C.mts):
        tiles = list(range(c0 // 128, (c0 + n) // 128))
        cs = slice(c0, c0 + n)
        k.dma("sp", yf[:, :, 0:n], S["yf"][:, cs].rearrange("(ft p) t -> p ft t", p=128), reads=[TT("yfb", 0, ti) for ti in tiles], writes=[TT("o_yf")])
        k.dma("sp", yb[:, :, 0:n], S["yb"][:, cs].rearrange("(ft p) t -> p ft t", p=128), reads=[TT("yfb", 1, ti) for ti in tiles], writes=[TT("o_yb")])
        k.dma("sp", uu[:, :, 0:n], S["zT"][6 * 128:10 * 128, cs].rearrange("(ft p) t -> p ft t", p=128),
              reads=[TT("zT", 6 + ft, mi) for ft in range(4)], writes=[TT("o_u")])
        _tt(k, "dve", yf[:, :, 0:n], yf[:, :, 0:n], yb[:, :, 0:n], ALU.add, [TT("o_yf"), TT("o_yb")], [TT("o_yf")])
        for ft in range(4):
            k.op("dve", lambda e, ft=ft, n=n: e.scalar_tensor_tensor(yf[:, ft, 0:n], uu[:, ft, 0:n], dsk2[:, ft:ft + 1], yf[:, ft, 0:n], ALU.mult, ALU.add),
                 [TT("o_u"), TT("o_yf"), TT("dsk2")], [TT("o_yf")])
        _tt(k, "pool", x2[:, :, 0:n], yf[:, :, 0:n], yf[:, :, 0:n], ALU.mult, [TT("o_yf")], [TT("o_x2")])
        k.op("pool", lambda e, n=n: e.tensor_scalar(x2[:, :, 0:n], x2[:, :, 0:n], c1_, c0_, ALU.mult, ALU.add), [TT("o_x2")], [TT("o_x2")])
        _tt(k, "pool", x2[:, :, 0:n], x2[:, :, 0:n], yf[:, :, 0:n], ALU.mult, [TT("o_x2"), TT("o_yf")], [TT("o_x2")])
        _act(k, sg[:, :, 0:n], x2[:, :, 0:n], AF.Sigmoid, [TT("o_x2")], [TT("o_sg")])
        _tt(k, "dve", g[:, :, 0:n], yf[:, :, 0:n], sg[:, :, 0:n], ALU.mult, [TT("o_yf"), TT("o_sg")], [TT("o_g")])
        for ot in range(4):
            pb = ot
            for kt in range(4):
                _mm(k, C.ps[pb][:, 0:n], wglu[:, kt, ot * 128:(ot + 1) * 128], g[:, kt, 0:n], kt == 0, kt == 3,
                    [TT("wglu"), TT("o_g")], [TT("ps", pb)])
            _act(k, s2[ot % 2][:, 0:n], C.ps[pb][:, 0:n], AF.Sigmoid, [TT("ps", pb), TT("bgl")], [TT("o_s2", ot % 2)], bias=bgl[:, ot:ot + 1])
            _tt(k, "dve", Ao[:, ot, 0:n], g[:, ot, 0:n], s2[ot % 2][:, 0:n], ALU.mult, [TT("o_g"), TT("o_s2", ot % 2)], [TT("o_A")])
        k.dma("pool", S["s5A"][:, cs].rearrange("(ft p) t -> p ft t", p=128), Ao[:, :, 0:n], reads=[TT("o_A")], writes=[TT("s5A", mi)])
    k.barrier()
    k.reset()


import os as _os
_GLA_CUT = int(_os.environ.get("GLA_CUT", "0"))


def _gla_end(C):
    C.k.barrier()
    C.k.reset()


def phase_gla(C, l):
    k, nc, I, S = C.k, C.nc, C.I, C.S
    TT = k.t
    NTILE = C.NTILE
    IDN, TRIU, TRIL, TRISU, TRISL = (slice(i * 128, (i + 1) * 128) for i in range(5))
    wa = k.sb([128, 256], BF16)
    ba = k.sb([128, 2, 256], BF16)
    k.dma("pool", wa[0:16, :], I["w_a2"][l, 0], writes=[TT("wa")])
    k.dma("pool", wa[32:48, :], I["w_a2"][l, 1], writes=[TT("wa")])
    k.dma("pool", ba[0:1, :, :], I["b_a2"][l].partition_broadcast(1), writes=[TT("ba")])
    k.dma("pool", ba[32:33, :, :], I["b_a2"][l].partition_broadcast(1), writes=[TT("ba")])
    gnm = k.sb([128, 4], F32)
    k.dma("sp", gnm[:], I["g_norm"][l].rearrange("(h p) -> p h", p=128), writes=[TT("gnm")], allow_slow_non_contiguous=True)
    Sf = [k.sb([128, 2, 128], F32) for _ in range(2)]
    Sb = [k.sb([128, 2, 128], BF16) for _ in range(2)]
    aT = k.sb([128, 128], BF16); gq = k.sb([128, 2, 128], BF16); gk = k.sb([128, 2, 128], BF16)
    gkv = k.sb([128, 768], BF16); gr = k.sb([128, 4, 128], BF16); ofl = k.sb([128, 4, 128], F32)
    e1 = k.sb([128, 256], F32); l1 = k.sb([128, 256], F32)
    eq = k.sb([128, 2, 128], F32); ek = k.sb([128, 2, 128], F32)
    qin = k.sb([128, 2, 128], BF16); kin = k.sb([128, 2, 128], BF16)
    attm = k.sb([128, 4, 128], BF16)
    ed = k.sb([128, 256], F32); kdec = k.sb([128, 256], BF16)
    osb = k.sb([128, 4, 128], F32); osq = k.sb([128, 4, 128], BF16); rs = k.sb([128, 512], F32)
    on = k.sb([128, 4, 128], F32); oa = k.sb([128, 4, 128], BF16)
    chain = [list(range(NTILE)), [1, 0] + list(range(NTILE - 1, 1, -1))]
    for d in range(2):
        tri_f = C.cst[:, TRIU] if d == 0 else C.cst[:, TRIL]
        tris_f = C.cst[:, TRISU] if d == 0 else C.cst[:, TRISL]
        mask_f = C.cst[:, TRIU] if d == 0 else C.cst[:, TRIL]
        last = 127 if d == 0 else 0
        ar = slice(d * 32, d * 32 + 16)
        k.op("dve", lambda e, d=d: e.memset(Sf[d][:], 0.0), [], [TT("gSf", d)])
        k.op("dve", lambda e, d=d: e.memset(Sb[d][:], 0.0), [], [TT("gSb", d)])
        for it in range(NTILE):
            ti = chain[d][it]
            mi = mt_of_tile(ti)
            cs = slice(ti * 128, (ti + 1) * 128)
            zT = S["zT"]
            k.dma("sp", aT[0:64, :], zT[4 * 128:4 * 128 + 64, cs], reads=[TT("zT", 4, mi)], writes=[TT("g_aT")])
            k.dma("sp", gq[:], zT[10 * 128:12 * 128, cs].rearrange("(a p) t -> p a t", p=128),
                  reads=[TT("zT", 10, mi), TT("zT", 11, mi)], writes=[TT("g_gq")])
            k.dma("sp", gk[:], zT[12 * 128:14 * 128, cs].rearrange("(a p) t -> p a t", p=128),
                  reads=[TT("zT", 12, mi), TT("zT", 13, mi)], writes=[TT("g_gk")])
            k.dma("sp", gkv[:], S["gkv"][cs, :], reads=[TT("gkv", ti)], writes=[TT("g_gkv")])
            if d == 1:
                k.dma("sp", gr[:], zT[14 * 128:18 * 128, cs].rearrange("(a p) t -> p a t", p=128),
                      reads=[TT("zT", 14 + a, mi) for a in range(4)], writes=[TT("g_gr")])
                k.dma("sp", ofl[:], S["of"][:, cs].rearrange("(a p) t -> p a t", p=128), reads=[TT("of", ti)], writes=[TT("g_of")])
            _mm(k, C.ps[0][:, 0:256], aT[ar, :], wa[ar, :], True, False, [TT("g_aT"), TT("wa")], [TT("ps", 0)])
            _mm(k, C.ps[0][:, 0:256], C.onesb[d * 32:d * 32 + 1, :], ba[d * 32:d * 32 + 1, d, :], False, True, [TT("onesb"), TT("ba")], [TT("ps", 0)])
            if _GLA_CUT == 1:
                return _gla_end(C)
            _act(k, e1[:], C.ps[0][:, 0:256], AF.Exp, [TT("ps", 0)], [TT("g_e1")], scale=-1.0)
            _act(k, l1[:], e1[:], AF.Ln, [TT("g_e1"), TT("one")], [TT("g_l1")], bias=C.oneT[:, 0:1])
            if _GLA_CUT == 2:
                return _gla_end(C)
            for pt in range(2):
                _mm(k, C.ps[1][:, pt * 128:(pt + 1) * 128], l1[:, pt * 128:(pt + 1) * 128], tri_f, True, True,
                    [TT("g_l1"), TT("cst")], [TT("ps", 1)])
            cps = C.ps[1][:, 0:256].rearrange("p (a t) -> p a t", a=2)
            _act(k, eq[:], cps, AF.Exp, [TT("ps", 1)], [TT("g_eq")], scale=-1.0 / 16)
            _act(k, ek[:], cps, AF.Exp, [TT("ps", 1)], [TT("g_ek")], scale=1.0 / 16)
            k.op("dve", lambda e: e.scalar_tensor_tensor(qin[:], gq[:], 0.125, eq[:], ALU.mult, ALU.mult),
                 [TT("g_gq"), TT("g_eq")], [TT("g_qin")])
            _tt(k, "pool", kin[:], gk[:], ek[:], ALU.mult, [TT("g_gk"), TT("g_ek")], [TT("g_kin")])
            if _GLA_CUT == 3:
                return _gla_end(C)
            for h in range(4):
                hr = slice((h % 2) * 64, (h % 2) * 64 + 64)
                pbk = 2 if h % 2 == 0 else 7
                _mm(k, C.ps[pbk][:, (h // 2) * 128:(h // 2 + 1) * 128], kin[hr, h // 2, :], qin[hr, h // 2, :], True, True,
                    [TT("g_kin"), TT("g_qin")], [TT("ps", pbk)])
            for par, pbk in ((0, 2), (1, 7)):
                _tt(k, "dve", attm[:, par::2, :], C.ps[pbk][:, 0:256].rearrange("p (h t) -> p h t", h=2),
                    mask_f.unsqueeze(1).to_broadcast([128, 2, 128]), ALU.mult, [TT("ps", pbk), TT("cst")], [TT("g_attm")])
            if _GLA_CUT == 5:
                return _gla_end(C)
            _mm(k, C.ps[3][:, 0:256], tris_f, l1[:], True, True, [TT("g_l1"), TT("cst")], [TT("ps", 3)])
            _act(k, ed[:], C.ps[3][:, 0:256], AF.Exp, [TT("ps", 3)], [TT("g_ed")], scale=-1.0 / 16)
            _tt(k, "pool", kdec[:], gkv[:, 0:256], ed[:], ALU.mult, [TT("g_gkv"), TT("g_ed")], [TT("g_kdec")])
            if _GLA_CUT == 6:
                return _gla_end(C)
            for pt in range(2):
                _mm(k, C.ps[4][:, pt * 256:(pt + 1) * 256], kdec[:, pt * 128:(pt + 1) * 128], gkv[:, 256 + pt * 256:256 + (pt + 1) * 256],
                    True, True, [TT("g_kdec"), TT("g_gkv")], [TT("ps", 4)])
            if _GLA_CUT == 7:
                return _gla_end(C)
            for h in range(4):
                hr = slice((h % 2) * 64, (h % 2) * 64 + 64)
                _mm(k, C.ps[5][:, h * 128:(h + 1) * 128], gkv[:, 256 + h * 128:256 + (h + 1) * 128], attm[:, h, :], True, False,
                    [TT("g_gkv"), TT("g_attm")], [TT("ps", 5)])
                _mm(k, C.ps[5][:, h * 128:(h + 1) * 128], Sb[d][hr, h // 2, :], qin[hr, h // 2, :], False, True,
                    [TT("gSb", d), TT("g_qin")], [TT("ps", 5)])
            if _GLA_CUT == 9:
                return _gla_end(C)
            for h in range(4):
                hr = slice((h % 2) * 64, (h % 2) * 64 + 64)
                pt = h // 2
                kvb = C.ps[4][hr, pt * 256 + (h % 2) * 128: pt * 256 + (h % 2) * 128 + 128]
                k.op("dve", lambda e, hr=hr, pt=pt, kvb=kvb, d=d, last=last: e.scalar_tensor_tensor(Sf[d][hr, pt, :], Sf[d][hr, pt, :], eq[hr, pt, last:last + 1], kvb, ALU.mult, ALU.add),
                     [TT("gSf", d), TT("g_eq"), TT("ps", 4)], [TT("gSf", d)])
            _copy(k, "pool", Sb[d][:], Sf[d][:], [TT("gSf", d)], [TT("gSb", d)])
            if _GLA_CUT == 8:
                return _gla_end(C)
            ops_ = C.ps[5][:, 0:512].rearrange("p (h t) -> p h t", h=4)
            if d == 0:
                _copy(k, "act", osb[:], ops_, [TT("ps", 5)], [TT("g_osb")])
                k.dma("pool", S["of"][:, cs].rearrange("(a p) t -> p a t", p=128), osb[:], reads=[TT("g_osb")], writes=[TT("of", ti)])
            else:
                _tt(k, "dve", osb[:], ops_, ofl[:], ALU.add, [TT("ps", 5), TT("g_of")], [TT("g_osb")])
                _act(k, osq[:], osb[:], AF.Square, [TT("g_osb")], [TT("g_osq")])
                _mm(k, C.ps[6][:, 0:512], C.onesb[:], osq[:].rearrange("p h t -> p (h t)"), True, True, [TT("g_osq"), TT("onesb")], [TT("ps", 6)])
                _act(k, rs[:], C.ps[6][:, 0:512], AF.Sqrt, [TT("ps", 6), TT("eps")], [TT("g_rs")], bias=C.epsT[:, 0:1], scale=1.0 / 128)
                k.op("dve", lambda e: e.reciprocal(rs[:], rs[:]), [TT("g_rs")], [TT("g_rs")])
                _tt(k, "dve", on[:], osb[:], rs[:].rearrange("p (h t) -> p h t", h=4), ALU.mult, [TT("g_osb"), TT("g_rs")], [TT("g_on")])
                for h in range(4):
                    k.op("dve", lambda e, h=h: e.scalar_tensor_tensor(oa[:, h, :], on[:, h, :], gnm[:, h:h + 1], gr[:, h, :], ALU.mult, ALU.mult),
                         [TT("g_on"), TT("gnm"), TT("g_gr")], [TT("g_oa")])
                k.dma("pool", S["glaA"][:, cs].rearrange("(a p) t -> p a t", p=128), oa[:], reads=[TT("g_oa")], writes=[TT("glaA", ti)])
    k.barrier()
    k.reset()


def phase_E1(C, l, last):
    k, nc, I, S = C.k, C.nc, C.I, C.S
    TT = k.t
    wo = [k.sb([128, 4, D], BF16) for _ in range(3)]
    for i, nm in enumerate(("w_o", "s_wo", "g_wo")):
        load_w_bf(C, wo[i][:], I[nm][l].rearrange("(kt p) c -> p kt c", p=128), TT("wo", i))
    wout = k.sb([128, 8, D], BF16)
    load_w_bf(C, wout[:], I["w_out"][l].rearrange("(kt p) c -> p kt c", p=128), TT("wout"), 2)
    br = [k.sb([128, 4, 512], BF16) for _ in range(3)]
    gt = k.sb([128, 24, 512], BF16)
    xt = k.sb([128, 8, 512], F32)
    mT = k.sb([128, 8, 512], BF16)
    ta = k.sb([128, 512], F32); tb = k.sb([128, 512], F32); tc_ = k.sb([128, 512], F32)
    xTv = S["xT"].rearrange("(kt p) t -> p kt t", p=128)
    for mi, (c0, n, isctx) in enumerate(C.mts):
        if last and isctx:
            continue
        cs = slice(c0, c0 + n)
        col = 1 if isctx else 0
        tiles = list(range(c0 // 128, (c0 + n) // 128))
        k.dma("sp", br[0][:, :, 0:n], S["attO"][:, cs].rearrange("(a p) t -> p a t", p=128), reads=[TT("attO", mi)], writes=[TT("e_br", 0)])
        k.dma("sp", br[1][:, :, 0:n], S["s5A"][:, cs].rearrange("(a p) t -> p a t", p=128), reads=[TT("s5A", mi)], writes=[TT("e_br", 1)])
        k.dma("sp", br[2][:, :, 0:n], S["glaA"][:, cs].rearrange("(a p) t -> p a t", p=128), reads=[TT("glaA", ti) for ti in tiles], writes=[TT("e_br", 2)])
        k.dma("sp", gt[:, :, 0:n], S["zT"][18 * 128:42 * 128, cs].rearrange("(a p) t -> p a t", p=128),
              reads=[TT("zT", 18 + a, mi) for a in range(24)], writes=[TT("e_gt")])
        k.dma("sp", xt[:, :, 0:n], xTv[:, :, cs], reads=[TT("xT", mi)], writes=[TT("e_xt")])
        for ot in range(8):
            for i in range(3):
                for kt in range(4):
                    _mm(k, C.ps[i][:, 0:n], wo[i][:, kt, ot * 128:(ot + 1) * 128], br[i][:, kt, 0:n], kt == 0, kt == 3,
                        [TT("wo", i), TT("e_br", i)], [TT("ps", i)])
            _tt(k, "dve", ta[:, 0:n], C.ps[0][:, 0:n], gt[:, ot, 0:n], ALU.mult, [TT("ps", 0), TT("e_gt")], [TT("e_ta")])
            _tt(k, "dve", tb[:, 0:n], C.ps[1][:, 0:n], gt[:, 8 + ot, 0:n], ALU.mult, [TT("ps", 1), TT("e_gt")], [TT("e_tb")])
            _tt(k, "dve", tc_[:, 0:n], C.ps[2][:, 0:n], gt[:, 16 + ot, 0:n], ALU.mult, [TT("ps", 2), TT("e_gt")], [TT("e_tc")])
            _tt(k, "pool", ta[:, 0:n], ta[:, 0:n], tb[:, 0:n], ALU.add, [TT("e_ta"), TT("e_tb")], [TT("e_ta")])
            _tt(k, "pool", mT[:, ot, 0:n], ta[:, 0:n], tc_[:, 0:n], ALU.add, [TT("e_ta"), TT("e_tc")], [TT("e_mT")])
        for ot in range(8):
            pb = 3 + ot % 4
            for kt in range(8):
                _mm(k, C.ps[pb][:, 0:n], wout[:, kt, ot * 128:(ot + 1) * 128], mT[:, kt, 0:n], kt == 0, kt == 7,
                    [TT("wout"), TT("e_mT")], [TT("ps", pb)])
            k.op("dve", lambda e, ot=ot, pb=pb, n=n, col=col: e.scalar_tensor_tensor(xt[:, ot, 0:n], C.ps[pb][:, 0:n], C.vec[:, 2, ot, col:col + 1], xt[:, ot, 0:n], ALU.mult, ALU.add),
                 [TT("ps", pb), TT("vec"), TT("e_xt")], [TT("e_xt")])
        k.dma("pool", xTv[:, :, cs], xt[:, :, 0:n], reads=[TT("e_xt")], writes=[TT("xT", mi)])
    k.barrier()
    k.reset()


def phase_E2(C, l, last):
    k, nc, I, S = C.k, C.nc, C.I, C.S
    TT = k.t
    wup = k.sb([128, 8, 5632], BF16)
    wdn = k.sb([128, 22, D], BF16)
    load_w_bf(C, wup[:], I["w_up"][l].rearrange("(kt p) c -> p kt c", p=128), TT("wup"), 8)
    load_w_bf(C, wdn[:], I["w_dn"][l].rearrange("(kt p) c -> p kt c", p=128), TT("wdn"), 2)
    xt = k.sb([128, 8, 512], F32)
    hT = k.sb([128, 8, 512], BF16)
    a = k.sb([128, 22, 512], BF16)
    rs = k.sb([128, 512], F32)
    tmp = [k.sb([128, 512], F32) for _ in range(2)]
    sgt = [k.sb([128, 512], F32) for _ in range(2)]
    fn = k.sb([128, 8], F32)
    k.dma("sp", fn[:], I["fnorm"].rearrange("(kt p) -> p kt", p=128), writes=[TT("fn")], allow_slow_non_contiguous=True)
    ident = C.cst[:, 0:128]
    xTv = S["xT"].rearrange("(kt p) t -> p kt t", p=128)
    for mi, (c0, n, isctx) in enumerate(C.mts):
        if last and isctx:
            continue
        cs = slice(c0, c0 + n)
        col = 1 if isctx else 0
        k.dma("sp", xt[:, :, 0:n], xTv[:, :, cs], reads=[TT("xT", mi)], writes=[TT("f_xt")])
        sq = a[:, 0:8, :]
        _act(k, sq[:, :, 0:n], xt[:, :, 0:n], AF.Square, [TT("f_xt")], [TT("f_a")])
        colsum_rstd(C, [sq[:, kt, 0:n] for kt in range(8)], n, rs, float(D), [TT("f_a")], TT("f_rs"))
        for kt in range(8):
            t = tmp[kt % 2]
            _tt(k, "dve", t[:, 0:n], xt[:, kt, 0:n], rs[:, 0:n], ALU.mult, [TT("f_xt"), TT("f_rs")], [TT("f_tmp", kt % 2)])
            _act(k, hT[:, kt, 0:n], t[:, 0:n], AF.Identity, [TT("f_tmp", kt % 2), TT("vec")], [TT("f_hT")],
                 bias=C.vec[:, 4, kt, col:col + 1], scale=C.vec[:, 3, kt, col:col + 1])
        for j in range(22):
            pg, pu = (2 * j) % 6, (2 * j + 1) % 6
            for kt in range(8):
                _mm(k, C.ps[pg][:, 0:n], wup[:, kt, j * 128:(j + 1) * 128], hT[:, kt, 0:n], kt == 0, kt == 7, [TT("wup"), TT("f_hT")], [TT("ps", pg)])
            for kt in range(8):
                _mm(k, C.ps[pu][:, 0:n], wup[:, kt, 2816 + j * 128:2816 + (j + 1) * 128], hT[:, kt, 0:n], kt == 0, kt == 7, [TT("wup"), TT("f_hT")], [TT("ps", pu)])
            _act(k, sgt[j % 2][:, 0:n], C.ps[pg][:, 0:n], AF.Silu, [TT("ps", pg)], [TT("f_sg", j % 2)])
            _tt(k, "dve", a[:, j, 0:n], sgt[j % 2][:, 0:n], C.ps[pu][:, 0:n], ALU.mult, [TT("f_sg", j % 2), TT("ps", pu)], [TT("f_a")])
        for ot in range(8):
            pb = 6 + ot % 2
            for j in range(22):
                _mm(k, C.ps[pb][:, 0:n], wdn[:, j, ot * 128:(ot + 1) * 128], a[:, j, 0:n], j == 0, j == 21, [TT("wdn"), TT("f_a")], [TT("ps", pb)])
            k.op("dve", lambda e, ot=ot, pb=pb, n=n, col=col: e.scalar_tensor_tensor(xt[:, ot, 0:n], C.ps[pb][:, 0:n], C.vec[:, 5, ot, col:col + 1], xt[:, ot, 0:n], ALU.mult, ALU.add),
                 [TT("ps", pb), TT("vec"), TT("f_xt")], [TT("f_xt")])
        if not last:
            k.dma("pool", xTv[:, :, cs], xt[:, :, 0:n], reads=[TT("f_xt")], writes=[TT("xT", mi)])
        else:
            sq = a[:, 0:8, :]
            _act(k, sq[:, :, 0:n], xt[:, :, 0:n], AF.Square, [TT("f_xt")], [TT("f_a")])
            colsum_rstd(C, [sq[:, kt, 0:n] for kt in range(8)], n, rs, float(D), [TT("f_a")], TT("f_rs"))
            for kt in range(8):
                k.op("dve", lambda e, kt=kt, n=n: e.scalar_tensor_tensor(xt[:, kt, 0:n], xt[:, kt, 0:n], fn[:, kt:kt + 1], rs[:, 0:n], ALU.mult, ALU.mult),
                     [TT("f_xt"), TT("fn"), TT("f_rs")], [TT("f_xt")])
            for j in range(n // 128):
                ob = j % 2
                for kt in range(8):
                    pb = kt % 6
                    k.op("pe", lambda e, o=C.ps[pb][:, 0:128], i=xt[:, kt, j * 128:(j + 1) * 128]: e.transpose(o, i, ident),
                         [TT("f_xt"), TT("cst")], [TT("ps", pb)])
                    dst = (sgt if kt < 4 else tmp)[ob][:, (kt % 4) * 128:(kt % 4 + 1) * 128]
                    tag = TT("f_sg", ob) if kt < 4 else TT("f_tmp", ob)
                    _copy(k, "act" if kt % 2 else "dve", dst, C.ps[pb][:, 0:128], [TT("ps", pb)], [tag])
                r0 = c0 - CTX + j * 128
                k.dma("pool", C.out[r0:r0 + 128, 0:512], sgt[ob][:], reads=[TT("f_sg", ob)], writes=[TT("out", r0, 0)])
                k.dma("pool", C.out[r0:r0 + 128, 512:1024], tmp[ob][:], reads=[TT("f_tmp", ob)], writes=[TT("out", r0, 1)])
    k.barrier()
    k.reset()


_NC_CACHE = {}


def _consts(T):
    cst = np.zeros((128, 1024), np.float32)
    m = np.arange(128)[:, None]
    l_ = np.arange(128)[None, :]
    cst[:, 0:128] = (m == l_)
    cst[:, 128:256] = (m <= l_)
    cst[:, 256:384] = (m >= l_)
    cst[:, 384:512] = (m > l_)
    cst[:, 512:640] = (m < l_)
    mk = np.zeros((128, 8), np.float32)
    for p in range(128):
        mk[p, (p // 32) * 2 + (p // 16) % 2] = 1.0
    rows_n = T // 64
    rows = np.repeat(np.arange(rows_n, dtype=np.float32), 64)
    cols = np.tile(np.arange(64, dtype=np.float32), rows_n)
    inv = (np.float32(10000.0) ** (-np.arange(8, dtype=np.float32) / np.float32(8))).astype(np.float32)
    ang = np.concatenate([rows[:, None] * inv, cols[:, None] * inv], axis=-1).astype(np.float32)
    cos, sin = np.cos(ang).astype(np.float32), np.sin(ang).astype(np.float32)
    cosR = np.repeat(cos.T, 2, axis=0)
    sinS = np.repeat(sin.T, 2, axis=0).copy()
    sinS[0::2] *= -1.0
    return cst, mk, np.ascontiguousarray(cosR), np.ascontiguousarray(sinS)


def _shared_inputs(inp, T):
    f = lambda a: np.ascontiguousarray(np.asarray(a, dtype=np.float32))
    w_in = f(inp["w_in"])
    L = w_in.shape[0]
    o = dict(cq=0, ckv=256, kr=512, u=544, gq=1056, gk=1312, gv=1568, gr=2080, af=2592, ab=2608, gates=2624)
    w_fm = np.zeros((L, 1024, NFM * 128), np.float32)
    w_fm[:, :, 0:256] = w_in[:, :, o["cq"]:o["cq"] + 256]
    w_fm[:, :, 256:512] = w_in[:, :, o["ckv"]:o["ckv"] + 256]
    w_fm[:, :, 512:528] = w_in[:, :, o["af"]:o["af"] + 16]
    w_fm[:, :, 544:560] = w_in[:, :, o["ab"]:o["ab"] + 16]
    w_fm[:, :, 576:608] = w_in[:, :, o["kr"]:o["kr"] + 32]
    swap = np.arange(32).reshape(16, 2)[:, ::-1].reshape(32)
    w_fm[:, :, 640 + 64:640 + 96] = w_in[:, :, o["kr"] + swap]
    w_fm[:, :, 768:1280] = w_in[:, :, o["u"]:o["u"] + 512]
    w_fm[:, :, 1280:1536] = w_in[:, :, o["gq"]:o["gq"] + 256]
    w_fm[:, :, 1536:1792] = w_in[:, :, o["gk"]:o["gk"] + 256]
    w_fm[:, :, 1792:2304] = w_in[:, :, o["gr"]:o["gr"] + 512]
    w_fm[:, :, 2304:5376] = w_in[:, :, o["gates"]:o["gates"] + 3072]
    w_tm = np.concatenate([w_in[:, :, o["gk"]:o["gk"] + 256], w_in[:, :, o["gv"]:o["gv"] + 512]], axis=2)
    w_uq = f(inp["mla_w_uq"])
    w_uqs = np.zeros_like(w_uq)
    for h in range(8):
        w_uqs[:, :, h * 96 + 64:h * 96 + 96] = w_uq[:, :, h * 96 + 64 + swap]
    cst, mk, cosR, sinS = _consts(T)
    sh = {
        "w_mod": f(inp["w_mod"]), "b_mod": f(inp["b_mod"]), "norm1": f(inp["norm1"]), "norm2": f(inp["norm2"]),
        "w_in": w_fm, "w_tm": np.ascontiguousarray(w_tm), "qn": f(inp["mla_q_norm"]), "kvn": f(inp["mla_kv_norm"]),
        "w_uq": w_uq, "w_uqs": w_uqs, "w_ukv": f(inp["mla_w_ukv"]), "w_o": f(inp["mla_w_o"]),
        "lam_re": f(inp["ssm_lam_re"]), "lam_im": f(inp["ssm_lam_im"]), "log_dt": f(inp["ssm_log_dt"]),
        "b_re": f(inp["ssm_b_re"]), "b_im": f(inp["ssm_b_im"]), "c_re": f(inp["ssm_c_re"]), "c_im": f(inp["ssm_c_im"]),
        "ssm_d": f(inp["ssm_d"]), "w_glu": f(inp["ssm_w_glu"]), "b_glu": f(inp["ssm_b_glu"]), "s_wo": f(inp["ssm_w_o"]),
        "w_a2": f(inp["gla_w_a2"]), "b_a2": f(inp["gla_b_a2"]), "g_norm": f(inp["gla_norm"]), "g_wo": f(inp["gla_w_o"]),
        "w_out": f(inp["w_out"]), "w_up": f(inp["ffn_w_up"]), "w_dn": f(inp["ffn_w_down"]), "fnorm": f(inp["final_norm"]),
        "cosR": cosR, "sinS": sinS, "cst": cst, "mk": mk,
    }
    return sh, L


def kernel(**inputs):
    x = np.asarray(inputs["x"], dtype=np.float32)
    B, T, _ = x.shape
    ctx = np.asarray(inputs["ctx"], dtype=np.float32)
    c = np.asarray(inputs["c"], dtype=np.float32)
    c_ctx = np.asarray(inputs["c_ctx"], dtype=np.float32)
    sh, L = _shared_inputs(inputs, T)
    key = (T, L)
    if key not in _NC_CACHE:
        _NC_CACHE[key] = build_program(T, L)
    nc = _NC_CACHE[key]
    in_maps = []
    for b in range(B):
        m = dict(sh)
        m["x"] = np.ascontiguousarray(x[b])
        m["ctx"] = np.ascontiguousarray(ctx[b])
        m["cc"] = np.ascontiguousarray(np.stack([c[b], c_ctx], axis=0))
        in_maps.append(m)
    res = run_bass_kernel_spmd(nc, in_maps, core_ids=list(range(B)))
    return np.stack([np.asarray(r["out"], dtype=np.float32) for r in res.results], axis=0)
```
